# Optimizing a Trainium2 kernel written in Bass

```python
import math
import jax, jax.numpy as jnp
from jax import lax
import numpy as np

D_MODEL = 2048
BATCH = 4
SEQ = 4096
DEPTH = 1

MEM_LEN = 256
EPS = 1e-6
HEAD_DIM_A = 128
WIDTH_A = (3 * D_MODEL) // 4
N_Q_HEADS_A = WIDTH_A // HEAD_DIM_A
N_KV_HEADS_A = N_Q_HEADS_A // 3
KV_WIDTH_A = N_KV_HEADS_A * HEAD_DIM_A
WINDOW = 128
BLOCK = 128
WIDTH_B = (3 * D_MODEL) // 4
CHUNK = 128
GROUP_DIM_B = 128
N_GROUPS_B = WIDTH_B // GROUP_DIM_B
N_HEADS_C = 4
WIDTH_C = D_MODEL // 2
HEAD_DIM_C = WIDTH_C // N_HEADS_C
N_BRANCH = 3
IN_SPLITS = [WIDTH_A, KV_WIDTH_A, KV_WIDTH_A, WIDTH_A,
             WIDTH_B, WIDTH_B, WIDTH_B,
             WIDTH_C, WIDTH_C,
             N_BRANCH * D_MODEL]
N_IN = sum(IN_SPLITS)
NEG_INF = -1e30

kernel_name = "hybrid_gated_window_gmlp_memxattn"


def rmsnorm(x, gain):
    xf = x.astype(jnp.float32)
    xf = xf * lax.rsqrt(jnp.mean(xf * xf, axis=-1, keepdims=True) + EPS)
    return (xf * gain.astype(jnp.float32)).astype(x.dtype)


def layernorm(x, gain, bias):
    xf = x.astype(jnp.float32)
    mu = jnp.mean(xf, axis=-1, keepdims=True)
    xc = xf - mu
    var = jnp.mean(xc * xc, axis=-1, keepdims=True)
    y = xc * lax.rsqrt(var + EPS) * gain.astype(jnp.float32) + bias.astype(jnp.float32)
    return y.astype(x.dtype)


def alibi_slopes(n):
    def pow2_slopes(m):
        start = 2.0 ** (-8.0 / m)
        return [start ** (i + 1) for i in range(m)]
    if math.log2(n).is_integer():
        s = pow2_slopes(n)
    else:
        c = 2 ** int(math.floor(math.log2(n)))
        s = pow2_slopes(c) + pow2_slopes(2 * c)[0::2][: n - c]
    return np.asarray(s, dtype=np.float32)


def windowed_gqa_sink(q, k, v, sink):
    B, S, Hq, Dh = q.shape
    Hkv = k.shape[2]
    rep = Hq // Hkv
    nb = S // BLOCK
    qb = q.reshape(B, nb, BLOCK, Hkv, rep, Dh)
    pad = ((0, 0), (BLOCK, BLOCK), (0, 0), (0, 0))
    kp = jnp.pad(k, pad).reshape(B, nb + 2, BLOCK, Hkv, Dh)
    vp = jnp.pad(v, pad).reshape(B, nb + 2, BLOCK, Hkv, Dh)
    kb = jnp.concatenate([kp[:, :-2], kp[:, 1:-1], kp[:, 2:]], axis=2)
    vb = jnp.concatenate([vp[:, :-2], vp[:, 1:-1], vp[:, 2:]], axis=2)
    scale = 1.0 / math.sqrt(Dh)
    scores = jnp.einsum('bnqhrd,bnkhd->bnhrqk', qb, kb).astype(jnp.float32) * scale
    t = jnp.arange(BLOCK)[:, None]
    j = jnp.arange(3 * BLOCK)[None, :]
    dist = jnp.abs(t - (j - BLOCK))
    key_pos = (jnp.arange(nb)[:, None] - 1) * BLOCK + jnp.arange(3 * BLOCK)[None, :]
    valid = (dist <= WINDOW)[None] & ((key_pos >= 0) & (key_pos < S))[:, None, :]
    slopes = jnp.asarray(alibi_slopes(Hq)).reshape(Hkv, rep)
    alibi = -slopes[:, :, None, None] * dist.astype(jnp.float32)[None, None]
    scores = scores + alibi[None, None]
    scores = jnp.where(valid[None, :, None, None], scores, NEG_INF)
    sink_col = jnp.broadcast_to(sink.astype(jnp.float32).reshape(1, 1, Hkv, rep, 1, 1),
                                scores.shape[:-1] + (1,))
    probs = jax.nn.softmax(jnp.concatenate([scores, sink_col], axis=-1), axis=-1)[..., :-1]
    out = jnp.einsum('bnhrqk,bnkhd->bnqhrd', probs.astype(v.dtype), vb)
    return out.reshape(B, S, Hq * Dh)


def chunked_spatial_gating(u, v, ln_gain, ln_bias, w_s, b_s):
    B, S, _ = u.shape
    u = jax.nn.gelu(u, approximate=False)
    v = layernorm(jax.nn.gelu(v, approximate=False), ln_gain, ln_bias)
    vc = v.reshape(B, S // CHUNK, CHUNK, N_GROUPS_B, GROUP_DIM_B)
    s = jnp.einsum('gts,bcsgd->bctgd', w_s, vc) + b_s.T[None, None, :, :, None]
    return u * s.reshape(B, S, WIDTH_B)


def memory_cross_attention(q, mem_h, w_kv_mem):
    B, S, _ = q.shape
    M = mem_h.shape[1]
    kv = mem_h @ w_kv_mem
    k, v = jnp.split(kv, 2, axis=-1)
    qh = q.reshape(B, S, N_HEADS_C, HEAD_DIM_C)
    kh = k.reshape(B, M, N_HEADS_C, HEAD_DIM_C)
    vh = v.reshape(B, M, N_HEADS_C, HEAD_DIM_C)
    scores = jnp.einsum('bshd,bmhd->bhsm', qh, kh).astype(jnp.float32) / math.sqrt(HEAD_DIM_C)
    probs = jax.nn.softmax(scores, axis=-1).astype(vh.dtype)
    out = jnp.einsum('bhsm,bmhd->bshd', probs, vh)
    return out.reshape(B, S, WIDTH_C)


def setup_inputs(seed: int = 0) -> dict:
    key = jax.random.key(seed)
    ks = jax.random.split(key, 20)
    f = jnp.float32
    nrm = lambda k, shape, scale: jax.random.normal(k, shape, f) * scale
    return {
        "x": nrm(ks[0], (BATCH, SEQ, D_MODEL), 1.0),
        "mem": nrm(ks[1], (BATCH, MEM_LEN, D_MODEL), 1.0),
        "norm_gain": 1.0 + nrm(ks[2], (DEPTH, D_MODEL), 0.02),
        "mem_norm_gain": 1.0 + nrm(ks[3], (DEPTH, D_MODEL), 0.02),
        "w_in": nrm(ks[4], (DEPTH, D_MODEL, N_IN), D_MODEL ** -0.5),
        "sink": nrm(ks[5], (DEPTH, N_Q_HEADS_A), 0.5),
        "ln_v_gain": 1.0 + nrm(ks[6], (DEPTH, WIDTH_B), 0.02),
        "ln_v_bias": nrm(ks[7], (DEPTH, WIDTH_B), 0.01),
        "w_spatial": nrm(ks[8], (DEPTH, N_GROUPS_B, CHUNK, CHUNK), CHUNK ** -0.5),
        "b_spatial": 1.0 + nrm(ks[9], (DEPTH, N_GROUPS_B, CHUNK), 0.01),
        "w_kv_mem": nrm(ks[10], (DEPTH, D_MODEL, 2 * WIDTH_C), D_MODEL ** -0.5),
        "w_br_a": nrm(ks[11], (DEPTH, WIDTH_A, D_MODEL), WIDTH_A ** -0.5),
        "w_br_b": nrm(ks[12], (DEPTH, WIDTH_B, D_MODEL), WIDTH_B ** -0.5),
        "w_br_c": nrm(ks[13], (DEPTH, WIDTH_C, D_MODEL), WIDTH_C ** -0.5),
        "w_out": nrm(ks[14], (DEPTH, D_MODEL, D_MODEL), D_MODEL ** -0.5),
        "final_gain": 1.0 + nrm(ks[15], (D_MODEL,), 0.02),
    }


def reference(x, mem, norm_gain, mem_norm_gain, w_in, sink, ln_v_gain, ln_v_bias,
              w_spatial, b_spatial, w_kv_mem, w_br_a, w_br_b, w_br_c, w_out, final_gain):
    B, S, D = x.shape
    split_idx = [int(c) for c in np.cumsum(IN_SPLITS)[:-1]]
    for l in range(DEPTH):
        h = rmsnorm(x, norm_gain[l])
        proj = h @ w_in[l]
        (q_a, k_a, v_a, z_a, u_b, v_b, z_b, q_c, z_c, gates) = jnp.split(proj, split_idx, axis=-1)
        o_a = windowed_gqa_sink(q_a.reshape(B, S, N_Q_HEADS_A, HEAD_DIM_A),
                                k_a.reshape(B, S, N_KV_HEADS_A, HEAD_DIM_A),
                                v_a.reshape(B, S, N_KV_HEADS_A, HEAD_DIM_A),
                                sink[l])
        o_a = o_a * jax.nn.silu(z_a)
        o_b = chunked_spatial_gating(u_b, v_b, ln_v_gain[l], ln_v_bias[l], w_spatial[l], b_spatial[l])
        o_b = o_b * jax.nn.silu(z_b)
        mem_h = rmsnorm(mem, mem_norm_gain[l])
        o_c = memory_cross_attention(q_c, mem_h, w_kv_mem[l]) * jax.nn.silu(z_c)
        g_a, g_b, g_c = jnp.split(jax.nn.sigmoid(gates), N_BRANCH, axis=-1)
        merged = (o_a @ w_br_a[l]) * g_a + (o_b @ w_br_b[l]) * g_b + (o_c @ w_br_c[l]) * g_c
        x = x + merged @ w_out[l]
    return rmsnorm(x, final_gain)
```

```python
import math
from contextlib import ExitStack

import numpy as np
import ml_dtypes

import concourse.bass as bass
import concourse.mybir as mybir
from concourse.bass_utils import run_bass_kernel_spmd

F32 = mybir.dt.float32
BF16 = mybir.dt.bfloat16
AF = mybir.ActivationFunctionType
ALU = mybir.AluOpType
AX = mybir.AxisListType

D = 2048
NCORES = 8
TT = 1024
NTB = 10
EPS = 1e-6
SCALE_A = 1.0 / math.sqrt(128.0)
SCALE_C = 1.0 / math.sqrt(256.0)
BIGNEG = -1.0e6
EDGE_NEG = -30000.0
C_Q, C_K, C_V, C_ZA = 0, 1536, 2048, 2560
C_UB, C_VB, C_ZB = 4096, 5632, 7168
C_QC, C_ZC, C_G = 8704, 9728, 10752

ENGS = ["pe", "act", "dve", "pool", "sp"]


def alibi_slopes(n):
    def pow2_slopes(m):
        start = 2.0 ** (-8.0 / m)
        return [start ** (i + 1) for i in range(m)]
    if math.log2(n).is_integer():
        s = pow2_slopes(n)
    else:
        c = 2 ** int(math.floor(math.log2(n)))
        s = pow2_slopes(c) + pow2_slopes(2 * c)[0::2][: n - c]
    return [float(v) for v in s]


class Sched:
    RING = 8

    def __init__(self, nc):
        self.nc = nc
        self.ops = {e: [] for e in ENGS}
        self.cw = {}
        self.cr = {}
        self.ndma = {e: 0 for e in ENGS}
        self.bank = 0

    def banks(self, n=1):
        if n == 2 and self.bank % 2:
            self.bank += 1
        b = self.bank % 8
        self.bank = (self.bank + n) % 8
        return b

    def op(self, eng, emit, r=(), w=(), dma=False):
        ops = self.ops[eng]
        tok = (eng, len(ops))
        deps = set()
        raw = set()
        for c in r:
            t = self.cw.get(c)
            if t is not None:
                deps.add(t)
                raw.add(t)
        for c in w:
            t = self.cw.get(c)
            if t is not None:
                deps.add(t)
            for t in self.cr.get(c, ()):
                deps.add(t)
        fdeps = set()
        for t in deps:
            if t[0] == eng and eng == "pe":
                continue
            fdeps.add(t)
        rec = dict(emit=emit, deps=fdeps, dma=dma, sig=dma, dn=None)
        if dma:
            n = self.ndma[eng]
            rec["dn"] = n
            self.ndma[eng] += 1
            if n >= self.RING:
                for j in range(len(ops) - 1, -1, -1):
                    if ops[j]["dma"] and ops[j]["dn"] == n - self.RING:
                        fdeps.add((eng, j))
                        break
        ops.append(rec)
        ws = set(w)
        for c in w:
            self.cw[c] = tok
            self.cr[c] = []
        for c in r:
            if c not in ws:
                self.cr.setdefault(c, []).append(tok)
        return tok

    def wait_all(self, eng, toks):
        self.ops[eng].append(dict(emit=None, deps=set(toks), dma=False, sig=False, dn=None))

    def finish(self, stack):
        nc = self.nc
        for e in ENGS:
            for rec in self.ops[e]:
                for (de, di) in rec["deps"]:
                    self.ops[de][di]["sig"] = True
        prog = {e: stack.enter_context(nc.semaphore("prog_" + e)) for e in ENGS}
        rings = {}
        for e in ENGS:
            if self.ndma[e]:
                rings[e] = [stack.enter_context(nc.semaphore("ring_%s_%d" % (e, i)))
                            for i in range(min(self.RING, self.ndma[e]))]
        for e in ENGS:
            cnt = 0
            for rec in self.ops[e]:
                if rec["dma"]:
                    n = rec["dn"]
                    rec["sv"] = (rings[e][n % self.RING], 16 * (n // self.RING + 1))
                elif rec["sig"]:
                    cnt += 1
                    rec["sv"] = (prog[e], cnt)
        block = stack.enter_context(nc.Block())

        def run(e):
            def body(h):
                known = {}
                for rec in self.ops[e]:
                    need = {}
                    for (de, di) in rec["deps"]:
                        sem, val = self.ops[de][di]["sv"]
                        k = id(sem)
                        if known.get(k, 0) >= val:
                            continue
                        if k not in need or need[k][1] < val:
                            need[k] = (sem, val)
                    for k, (sem, val) in need.items():
                        h.wait_ge(sem, val)
                        known[k] = val
                    if rec["emit"] is None:
                        continue
                    ins = rec["emit"](h)
                    if rec["dma"]:
                        ins.then_inc(rec["sv"][0], 16)
                    elif rec["sig"]:
                        ins.then_inc(rec["sv"][0], 1)
            return body

        block.tensor(run("pe"))
        block.scalar(run("act"))
        block.vector(run("dve"))
        block.gpsimd(run("pool"))
        block.sync(run("sp"))


def build_program():
    nc = bass.Bass("TRN2", target_bir_lowering=False)

    def din(name, shape, dt=F32):
        return nc.dram_tensor(name, list(shape), dt, kind="ExternalInput").ap()

    xp = din("xp", [2304, D])
    memd = din("mem", [256, D])
    w_in = din("w_in", [D, 16896])
    w_kv = din("w_kv", [D, 2048])
    w_bra = din("w_bra", [1536, D])
    w_brb = din("w_brb", [1536, D])
    w_brc = din("w_brc", [1024, D])
    w_out = din("w_out", [D, D])
    ng_d = din("ng", [1, D])
    mg_d = din("mg", [1, D])
    fg_d = din("fg", [1, D])
    sink_d = din("sink", [1, 12])
    lng_d = din("lng", [128, 12])
    lnb_d = din("lnb", [128, 12])
    wsT_d = din("wsT", [128, 1536])
    bs_d = din("bs", [1, 1536])
    edge_d = din("edge", [128, 4])
    nd_d = din("nd", [128, 384], BF16)
    ident_d = din("ident", [128, 128], BF16)
    y = nc.dram_tensor("y", [2048, D], F32, kind="ExternalOutput").ap()

    slopes = alibi_slopes(12)

    with ExitStack() as st:
        def sb(name, shape, dt):
            return st.enter_context(nc.sbuf_tensor(name, list(shape), dt))

        hT = sb("hT", [128, 16, NTB * 128], BF16)
        OT = sb("OT", [128, 32768], BF16)
        AR = sb("AR", [128, 18432], BF16)
        KcT = sb("KcT", [128, 8, 256], BF16)
        Vc = sb("Vc", [128, 2, 1024], BF16)
        W = [sb("W0", [128, 16, 384], BF16), sb("W1", [128, 16, 384], BF16)]
        WB = sb("WB", [128, 32, 128], BF16)
        GN = sb("GN", [128, 2048], F32)
        ND = sb("ND", [128, 384], BF16)
        IDT = sb("IDT", [128, 128], BF16)
        ONES = sb("ONES", [128, 128], BF16)
        WST = sb("WST", [128, 12, 128], BF16)
        CC = sb("CC", [128, 12, 128], F32)
        SM = sb("SM", [128, 512], F32)
        ps = st.enter_context(nc.psum_tensor("ps", [128, 4096], F32))

        S = Sched(nc)

        def arv(off, cols, dt, extra=None):
            nb = cols * (4 if dt == F32 else 2)
            v = AR[:, off // 2:(off + nb) // 2]
            if dt == F32:
                v = v.bitcast(F32)
            return v

        def arc(off, nbytes):
            return ["AR%d" % i for i in range(off // 1024, (off + nbytes + 1023) // 1024)]

        def gnc(off, nbytes):
            return ["GN%d" % i for i in range(off // 1024, (off + nbytes + 1023) // 1024)]

        def otc(off, nbytes):
            return ["OT%d" % i for i in range(off // 1024, (off + nbytes + 1023) // 1024)]

        GN_ALL = gnc(0, 8192)

        def ot_slab(chunk, s):
            e0 = chunk * 1024 + s * 512
            return OT[:, e0:e0 + 512], otc(e0 * 2, 1024)

        def psb(bank, n, off=0):
            return ps[:, bank * 512 + off: bank * 512 + off + n]

        smn = [0]

        def smalloc(n):
            c0 = smn[0]
            smn[0] += n
            assert smn[0] <= 512
            return c0

        WC = [["W0.0", "W0.1", "W0.2"], ["W1.0", "W1.1", "W1.2"]]

        def load_w(slot, segs):
            off = 0
            for i, (src, c0, ncol) in enumerate(segs):
                dst = W[slot][:, :, off:off + ncol]
                srcv = src[:, c0:c0 + ncol].rearrange("(kc p) n -> p kc n", p=128)
                S.op("pool", lambda h, d=dst, s_=srcv: h.dma_start(out=d, in_=s_),
                     w=[WC[slot][i]], dma=True)
                off += ncol

        def mm(out_ap, pairs, r, w):
            def emit(h):
                n = len(pairs)
                ins = None
                for i, (l, rh) in enumerate(pairs):
                    ins = h.matmul(out_ap, lhsT=l, rhs=rh, start=(i == 0), stop=(i == n - 1))
                return ins
            return S.op("pe", emit, r=r, w=w)

        cpy_rr = [0]

        def copy_out(dst, src, r, w, eng=None):
            if eng is None:
                eng = "act" if cpy_rr[0] % 2 == 0 else "dve"
                cpy_rr[0] += 1
            if eng == "act":
                S.op("act", lambda h: h.activation(out=dst, in_=src, func=AF.Copy), r=r, w=w)
            else:
                S.op("dve", lambda h: h.tensor_copy(out=dst, in_=src), r=r, w=w)

        c_eps = smalloc(1)
        c_es = smalloc(12)
        c_sk = smalloc(12)
        c_lng = smalloc(12)
        c_lnb = smalloc(12)
        c_edge = smalloc(4)
        S.op("dve", lambda h: h.memset(SM[:, c_eps:c_eps + 1], EPS), w=["eps"])
        S.op("dve", lambda h: h.memset(ONES[:], 1.0), w=["ones"])
        S.op("sp", lambda h: h.dma_start(out=IDT[:], in_=ident_d[:, :]), w=["idt"], dma=True)
        S.op("sp", lambda h: h.dma_start(out=ND[:], in_=nd_d[:, :]), w=["nd"], dma=True)
        S.op("sp", lambda h: h.dma_start(out=SM[:, c_edge:c_edge + 4], in_=edge_d[:, :]), w=["edge"], dma=True)
        S.op("sp", lambda h: h.dma_start(out=SM[:, c_lng:c_lng + 12], in_=lng_d[:, :]), w=["lng"], dma=True)
        S.op("sp", lambda h: h.dma_start(out=SM[:, c_lnb:c_lnb + 12], in_=lnb_d[:, :]), w=["lnb"], dma=True)
        S.op("sp", lambda h: h.dma_start(out=SM[:, c_sk:c_sk + 12],
                                         in_=sink_d[0:1, :].broadcast_to([128, 12])), w=["sk"], dma=True)
        S.op("sp", lambda h: h.dma_start(out=CC[:].rearrange("p g t -> p (g t)"),
                                         in_=bs_d[0:1, :].broadcast_to([128, 1536])), w=["cc"], dma=True)
        S.op("pool", lambda h: h.dma_start(out=WST[:].rearrange("p g t -> p (g t)"), in_=wsT_d[:, :]),
             w=["wst"], dma=True)
        S.op("act", lambda h: h.activation(out=SM[:, c_es:c_es + 12], in_=SM[:, c_sk:c_sk + 12], func=AF.Exp),
             r=["sk"], w=["es"])
        for k in range(3):
            b = S.banks()
            mm(psb(b, 512), [(ONES[:], WST[:].rearrange("p g t -> p (g t)")[:, k * 512:(k + 1) * 512])],
               r=["ones", "wst"], w=["P%d" % b])
            for gg in range(4):
                g = k * 4 + gg
                S.op("dve", lambda h, g=g, gg=gg, b=b: h.scalar_tensor_tensor(
                    out=CC[:, g, :], in0=psb(b, 128, gg * 128), scalar=SM[:, c_lnb + g:c_lnb + g + 1],
                    in1=CC[:, g, :], op0=ALU.mult, op1=ALU.add),
                    r=["lnb"], w=["cc", "P%d" % b])

        XS_OFF = [0, 8192, 16384]
        HN_OFF = [24576, 28672, 32768]
        rms_n = [0]

        def rms_block(src_rows, dstT, dst_cells):
            i = rms_n[0] % 3
            rms_n[0] += 1
            xs = arv(XS_OFF[i], 2048, F32)
            hn = arv(HN_OFF[i], 2048, BF16)
            xsc = arc(XS_OFF[i], 8192)
            hnc = arc(HN_OFF[i], 4096)
            if i not in rms_cols:
                rms_cols[i] = smalloc(3)
            c = rms_cols[i]
            stc = "rst%d" % i
            S.op("sp", lambda h: h.dma_start(out=xs, in_=src_rows), w=xsc, dma=True)
            S.op("act", lambda h: h.activation(out=hn, in_=xs, func=AF.Square, accum_out=SM[:, c:c + 1]),
                 r=xsc, w=hnc + [stc])
            S.op("act", lambda h: h.activation(out=SM[:, c + 1:c + 2], in_=SM[:, c:c + 1], func=AF.Sqrt,
                                               scale=1.0 / D, bias=SM[:, c_eps:c_eps + 1]),
                 r=[stc, "eps"], w=[stc])
            S.op("dve", lambda h: h.reciprocal(out=SM[:, c + 2:c + 3], in_=SM[:, c + 1:c + 2]), r=[stc], w=[stc])
            S.op("dve", lambda h: h.scalar_tensor_tensor(out=hn, in0=xs, scalar=SM[:, c + 2:c + 3], in1=GN[:],
                                                         op0=ALU.mult, op1=ALU.mult),
                 r=xsc + [stc] + GN_ALL, w=hnc)
            b = S.banks(2)
            tp = ps[:, b * 512:b * 512 + 1024].bitcast(BF16)

            def tr(h):
                ins = None
                for kc in range(16):
                    ins = h.transpose(out=tp[:, kc * 128:(kc + 1) * 128], in_=hn[:, kc * 128:(kc + 1) * 128],
                                      identity=IDT[:])
                return ins
            pc = ["P%d" % b, "P%d" % (b + 1)]
            S.op("pe", tr, r=hnc + ["idt"], w=pc)

            def fin():
                copy_out(dstT, tp.rearrange("p (k t) -> p k t", t=128), r=[], w=dst_cells + pc)
            return fin

        rms_cols = {}

        def load_gain(src):
            S.op("sp", lambda h: h.dma_start(out=GN[:], in_=src[0:1, :].broadcast_to([128, D])),
                 w=GN_ALL, dma=True)

        stages = []

        memT = WB[:].rearrange("p a b -> p (a b)").rearrange("p (k t) -> p k t", t=256)
        memT_c = ["WB0", "WB1", "WB2"]

        def mem_prep(slot):
            load_gain(mg_d)
            pend = None
            for mb in range(2):
                f = rms_block(memd[mb * 128:(mb + 1) * 128, :], memT[:, :, mb * 128:(mb + 1) * 128], memT_c)
                if pend is not None:
                    pend()
                pend = f
            pend()

        def make_kvK(c0, ncol):
            def f(slot):
                for cc in range(ncol // 128):
                    b = S.banks()
                    mm(psb(b, 256), [(W[slot][:, kc, cc * 128:(cc + 1) * 128], memT[:, kc, :]) for kc in range(16)],
                       r=WC[slot] + memT_c, w=["P%d" % b])
                    ch = (c0 + cc * 128) // 128
                    copy_out(KcT[:, ch, :], psb(b, 256), r=[], w=["kct%d" % ch, "P%d" % b])
            return f

        def make_kvV(c0, ncol):
            def f(slot):
                for mb in range(2):
                    b = S.banks()
                    mm(psb(b, ncol), [(memT[:, kc, mb * 128:(mb + 1) * 128], W[slot][:, kc, 0:ncol])
                                      for kc in range(16)],
                       r=WC[slot] + memT_c, w=["P%d" % b])
                    copy_out(Vc[:, mb, c0 - 1024:c0 - 1024 + ncol], psb(b, ncol), r=[],
                             w=["vc%d_%d" % (mb, c0), "P%d" % b])
            return f
        VC_CELLS = ["vc%d_%d" % (mb, c0) for mb in range(2) for c0 in (1024, 1408, 1792)]
        KCT_CELLS = ["kct%d" % i for i in range(8)]
        kvst = []
        for (c0, ncol) in ((0, 384), (384, 384), (768, 256)):
            kvst.append(([(w_kv, c0, ncol)], make_kvK(c0, ncol)))
        for (c0, ncol) in ((1024, 384), (1408, 384), (1792, 256)):
            kvst.append(([(w_kv, c0, ncol)], make_kvV(c0, ncol)))

        HT_CELLS = ["hT%d" % i for i in range(NTB)]
        CORE_SLABS = [(128, ["hT1", "hT2", "hT3", "hT4"]), (640, ["hT5", "hT6", "hT7", "hT8"])]

        V_OFF, KT_OFF, QT_OFF, E_OFF, TMP_OFF = 0, 10240, 20480, 24576, (30720, 32768)
        Vt = arv(V_OFF, 5120, BF16).rearrange("p (b c) -> p b c", c=512)
        kTt = arv(KT_OFF, 5120, BF16).rearrange("p (g t) -> p g t", t=1280)
        RR_OFF = 0
        rr = GN[:, 0:512]
        rr_c = gnc(0, 2048)
        QB0 = [max(1, j - 1) for j in range(NTB)]
        QB1 = [min(8, j + 1) for j in range(NTB)]
        NJ = [(QB1[j] - QB0[j] + 1) * 128 for j in range(NTB)]
        EO = [sum(NJ[:j]) for j in range(NTB)]

        def tile_stages(tt):
            r0 = tt * 1024
            p0l, zal, zcl, zbl, rest = [], [], [], [], []

            def phase0(slot):
                load_gain(ng_d)
                pend = None
                for tb in range(NTB):
                    f = rms_block(xp[r0 + tb * 128:r0 + (tb + 1) * 128, :], hT[:, :, tb * 128:(tb + 1) * 128],
                                  ["hT%d" % tb])
                    if pend is not None:
                        pend()
                    pend = f
                pend()
            p0l.append((None, phase0))

            def make_z(chunk0, nch):
                def f(slot):
                    for cc in range(nch):
                        for s, (t0, hc) in enumerate(CORE_SLABS):
                            b = S.banks()
                            mm(psb(b, 512), [(W[slot][:, kc, cc * 128:(cc + 1) * 128], hT[:, kc, t0:t0 + 512])
                                             for kc in range(16)], r=WC[slot] + hc, w=["P%d" % b])
                            dst, dc = ot_slab(chunk0 + cc, s)
                            S.op("act", lambda h, dst=dst, b=b: h.activation(out=dst, in_=psb(b, 512), func=AF.Silu),
                                 r=[], w=dc + ["P%d" % b])
                return f
            for k in range(4):
                zal.append(([(w_in, C_ZA + k * 384, 384)], make_z(3 * k, 3)))
            for (c0, ncol) in ((0, 384), (384, 384), (768, 256)):
                zcl.append(([(w_in, C_ZC + c0, ncol)], make_z(24 + c0 // 128, ncol // 128)))
            for k in range(4):
                zbl.append(([(w_in, C_ZB + k * 384, 384)], make_z(12 + 3 * k, 3)))

            def make_v(c0, ncol):
                def f(slot):
                    for tb in range(NTB):
                        b = S.banks()
                        mm(psb(b, ncol), [(hT[:, kc, tb * 128:(tb + 1) * 128], W[slot][:, kc, 0:ncol])
                                          for kc in range(16)], r=WC[slot] + ["hT%d" % tb], w=["P%d" % b])
                        copy_out(Vt[:, tb, c0:c0 + ncol], psb(b, ncol), r=[],
                                 w=arc(V_OFF + tb * 1024 + c0 * 2, ncol * 2) + ["P%d" % b])
                return f
            rest.append(([(w_in, C_V, 384)], make_v(0, 384)))
            rest.append(([(w_in, C_V + 384, 128)], make_v(384, 128)))

            def make_k(g0, ng):
                def f(slot):
                    for gg in range(ng):
                        g = g0 + gg
                        for (t0, n, hc) in ((0, 512, HT_CELLS[0:4]), (512, 512, HT_CELLS[4:8]),
                                            (1024, 256, HT_CELLS[8:10])):
                            b = S.banks()
                            mm(psb(b, n), [(W[slot][:, kc, gg * 128:(gg + 1) * 128], hT[:, kc, t0:t0 + n])
                                           for kc in range(16)], r=WC[slot] + hc, w=["P%d" % b])
                            copy_out(kTt[:, g, t0:t0 + n], psb(b, n), r=[],
                                     w=arc(KT_OFF + g * 2560 + t0 * 2, n * 2) + ["P%d" % b])
                return f
            rest.append(([(w_in, C_K, 384)], make_k(0, 3)))
            rest.append(([(w_in, C_K + 384, 128)], make_k(3, 1)))

            V_ALL = arc(V_OFF, 10240)
            KT_ALL = arc(KT_OFF, 10240)
            E_OFFS = (24576, 30720)
            rrs = [GN[:, 0:512], GN[:, 512:1024]]
            rrs_c = [gnc(0, 2048), gnc(2048, 2048)]

            def pv(h):
                g = h // 3
                eoff = E_OFFS[h % 2]
                Et = arv(eoff, 3072, BF16)
                E_ALL = arc(eoff, 6144)
                for s in range(2):
                    bo = S.banks()
                    br = S.banks()

                    def emit(hh, s=s, bo=bo, br=br, g=g, Et=Et):
                        ins = None
                        for qi in range(4):
                            i = 1 + 4 * s + qi
                            js = [i - 1, i, i + 1]
                            for idx, j in enumerate(js):
                                e0 = EO[j] + (i - QB0[j]) * 128
                                ins = hh.matmul(psb(bo, 128, qi * 128), lhsT=Vt[:, j, g * 128:(g + 1) * 128],
                                                rhs=Et[:, e0:e0 + 128], start=(idx == 0), stop=(idx == 2))
                        for qi in range(4):
                            i = 1 + 4 * s + qi
                            js = [i - 1, i, i + 1]
                            for idx, j in enumerate(js):
                                e0 = EO[j] + (i - QB0[j]) * 128
                                ins = hh.matmul(psb(br, 128, qi * 128), lhsT=ONES[:],
                                                rhs=Et[:, e0:e0 + 128], start=(idx == 0), stop=(idx == 2))
                        return ins
                    S.op("pe", emit, r=V_ALL + E_ALL + ["ones"], w=["P%d" % bo, "P%d" % br])
                    oa, oac = ot_slab(h, s)
                    rr_, rrc_ = rrs[s], rrs_c[s]
                    S.op("act", lambda hh, br=br, h=h, rr_=rr_: hh.activation(
                        out=rr_, in_=psb(br, 512), func=AF.Identity, bias=SM[:, c_es + h:c_es + h + 1]),
                        r=["es"], w=rrc_ + ["P%d" % br])
                    S.op("dve", lambda hh, rr_=rr_: hh.reciprocal(out=rr_, in_=rr_), r=rrc_, w=rrc_)
                    S.op("dve", lambda hh, oa=oa, rr_=rr_: hh.tensor_tensor(out=rr_, in0=rr_, in1=oa, op=ALU.mult),
                         r=rrc_ + oac, w=rrc_)
                    S.op("dve", lambda hh, oa=oa, bo=bo, rr_=rr_: hh.tensor_tensor(out=oa, in0=psb(bo, 512), in1=rr_,
                                                                                   op=ALU.mult),
                         r=rrc_, w=oac + ["P%d" % bo])

            def make_head(h):
                def f(slot):
                    g = h // 3
                    qoff = QT_OFF + (h % 2) * 2048
                    qT = arv(qoff, 1024, BF16)
                    eoff = E_OFFS[h % 2]
                    Et = arv(eoff, 3072, BF16)
                    qs = SCALE_A / slopes[h]
                    for s, (t0, hc) in enumerate(CORE_SLABS):
                        b = S.banks()
                        mm(psb(b, 512), [(W[slot][:, kc, 0:128], hT[:, kc, t0:t0 + 512]) for kc in range(16)],
                           r=WC[slot] + hc, w=["P%d" % b])
                        S.op("act", lambda hh, b=b, s=s: hh.activation(out=qT[:, s * 512:(s + 1) * 512],
                                                                       in_=psb(b, 512), func=AF.Identity, scale=qs),
                             r=[], w=arc(qoff + s * 1024, 1024) + ["P%d" % b])
                    qc_all = arc(qoff, 2048)
                    for j in range(NTB):
                        b = S.banks()
                        n = NJ[j]
                        q0 = (QB0[j] - 1) * 128
                        nd0 = (QB0[j] - j + 1) * 128
                        mm(psb(b, n), [(kTt[:, g, j * 128:(j + 1) * 128], qT[:, q0:q0 + n]),
                                       (IDT[:], ND[:, nd0:nd0 + n])],
                           r=KT_ALL + qc_all + ["idt", "nd"], w=["P%d" % b])
                        ecells = arc(eoff + EO[j] * 2, n * 2)
                        if j == 0 or j == NTB - 1:
                            ecol = c_edge + tt * 2 + (0 if j == 0 else 1)
                            S.op("act", lambda hh, b=b, j=j, n=n, ecol=ecol: hh.activation(
                                out=Et[:, EO[j]:EO[j] + n], in_=psb(b, n), func=AF.Exp, scale=slopes[h],
                                bias=SM[:, ecol:ecol + 1]), r=["edge"], w=ecells + ["P%d" % b])
                        else:
                            S.op("act", lambda hh, b=b, j=j, n=n: hh.activation(
                                out=Et[:, EO[j]:EO[j] + n], in_=psb(b, n), func=AF.Exp, scale=slopes[h]),
                                r=[], w=ecells + ["P%d" % b])
                    if h > 0:
                        pv(h - 1)
                    if h == 11:
                        pv(11)
                return f
            for h in range(12):
                rest.append(([(w_in, C_Q + h * 128, 128)], make_head(h)))

            QC_OFF, EC_OFF = 0, 4096
            qcT = arv(QC_OFF, 2048, BF16).rearrange("p (d t) -> p d t", t=1024)
            EcT = arv(EC_OFF, 2048, BF16).rearrange("p (m t) -> p m t", t=1024)
            rr2 = GN[:, 512:1024]
            rr2_c = gnc(2048, 2048)

            def make_chead(hc_):
                def f(slot):
                    for dc in range(2):
                        for s, (t0, hc) in enumerate(CORE_SLABS):
                            b = S.banks()
                            mm(psb(b, 512), [(W[slot][:, kc, dc * 128:(dc + 1) * 128], hT[:, kc, t0:t0 + 512])
                                             for kc in range(16)], r=WC[slot] + hc, w=["P%d" % b])
                            copy_out(qcT[:, dc, s * 512:(s + 1) * 512], psb(b, 512), r=[],
                                     w=arc(QC_OFF + dc * 2048 + s * 1024, 1024) + ["P%d" % b])
                    qc_cells = arc(QC_OFF, 4096)
                    for mb in range(2):
                        for s in range(2):
                            b = S.banks()
                            mm(psb(b, 512), [(KcT[:, hc_ * 2 + dc, mb * 128:(mb + 1) * 128],
                                              qcT[:, dc, s * 512:(s + 1) * 512]) for dc in range(2)],
                               r=KCT_CELLS + qc_cells, w=["P%d" % b])
                            S.op("act", lambda hh, b=b, mb=mb, s=s: hh.activation(
                                out=EcT[:, mb, s * 512:(s + 1) * 512], in_=psb(b, 512), func=AF.Exp,
                                scale=SCALE_C), r=[], w=arc(EC_OFF + mb * 2048 + s * 1024, 1024) + ["P%d" % b])
                    ec_cells = arc(EC_OFF, 4096)
                    for s in range(2):
                        br = S.banks()
                        mm(psb(br, 512), [(ONES[:], EcT[:, mb, s * 512:(s + 1) * 512]) for mb in range(2)],
                           r=["ones"] + ec_cells, w=["P%d" % br])
                        S.op("dve", lambda hh, br=br: hh.reciprocal(out=rr, in_=psb(br, 512)), r=[],
                             w=rr_c + ["P%d" % br])
                        for dc in range(2):
                            bo = S.banks()
                            mm(psb(bo, 512), [(Vc[:, mb, hc_ * 256 + dc * 128: hc_ * 256 + (dc + 1) * 128],
                                               EcT[:, mb, s * 512:(s + 1) * 512]) for mb in range(2)],
                               r=VC_CELLS + ec_cells, w=["P%d" % bo])
                            oc, occ = ot_slab(24 + hc_ * 2 + dc, s)
                            S.op("dve", lambda hh, oc=oc: hh.tensor_tensor(out=rr2, in0=rr, in1=oc, op=ALU.mult),
                                 r=rr_c + occ, w=rr2_c)
                            S.op("dve", lambda hh, oc=oc, bo=bo: hh.tensor_tensor(out=oc, in0=psb(bo, 512), in1=rr2,
                                                                                 op=ALU.mult),
                                 r=rr2_c, w=occ + ["P%d" % bo])
                return f
            for hc_ in range(4):
                rest.append(([(w_in, C_QC + hc_ * 256, 256)], make_chead(hc_)))

            GB_OFF = 0
            gbuf = arv(GB_OFF, 12288, BF16).rearrange("p (b c) -> p b c", c=1536)
            UG_OFF = (24576, 25600)
            T1_OFF = (26624, 28672)
            SQ_OFF = 30720
            sqj = arv(SQ_OFF, 512, BF16)
            c_s1 = smalloc(32)
            c_s2 = smalloc(32)
            c_bst = smalloc(48)

            def make_vb(k):
                def f(slot):
                    for tb in range(8):
                        b = S.banks()
                        mm(psb(b, 384), [(hT[:, kc, (tb + 1) * 128:(tb + 2) * 128], W[slot][:, kc, 0:384])
                                         for kc in range(16)], r=WC[slot] + ["hT%d" % (tb + 1)], w=["P%d" % b])
                        gsl = gbuf[:, tb, k * 384:(k + 1) * 384]
                        gcl = arc(GB_OFF + tb * 3072 + k * 768, 768)
                        col = tb * 4 + k
                        S.op("act", lambda hh, gsl=gsl, b=b, col=col: hh.activation(
                            out=gsl, in_=psb(b, 384), func=AF.Gelu, accum_out=SM[:, c_s1 + col:c_s1 + col + 1]),
                            r=[], w=gcl + ["P%d" % b, "s1_%d" % col])
                        S.op("act", lambda hh, gsl=gsl, col=col: hh.activation(
                            out=sqj[:, 0:384], in_=gsl, func=AF.Square, accum_out=SM[:, c_s2 + col:c_s2 + col + 1]),
                            r=gcl, w=arc(SQ_OFF, 1024) + ["s2_%d" % col])
                return f
            for k in range(4):
                rest.append(([(w_in, C_VB + k * 384, 384)], make_vb(k)))

            S1C = ["s1_%d" % i for i in range(32)]
            S2C = ["s2_%d" % i for i in range(32)]

            def make_ub(k):
                def f(slot):
                    if k == 0:
                        m_ = SM[:, c_bst:c_bst + 8]
                        q_ = SM[:, c_bst + 8:c_bst + 16]
                        v_ = SM[:, c_bst + 16:c_bst + 24]
                        r_ = SM[:, c_bst + 24:c_bst + 32]
                        n_ = SM[:, c_bst + 32:c_bst + 40]
                        S.op("dve", lambda hh: hh.tensor_reduce(
                            out=m_, in_=SM[:, c_s1:c_s1 + 32].rearrange("p (b k) -> p b k", k=4), axis=AX.X,
                            op=ALU.add), r=S1C, w=["bst_m"])
                        S.op("dve", lambda hh: hh.tensor_reduce(
                            out=q_, in_=SM[:, c_s2:c_s2 + 32].rearrange("p (b k) -> p b k", k=4), axis=AX.X,
                            op=ALU.add), r=S2C, w=["bst_q"])
                        S.op("dve", lambda hh: hh.tensor_scalar(out=m_, in0=m_, scalar1=1.0 / 1536, scalar2=None,
                                                                op0=ALU.mult), r=["bst_m"], w=["bst_m"])
                        S.op("dve", lambda hh: hh.tensor_tensor(out=v_, in0=m_, in1=m_, op=ALU.mult),
                             r=["bst_m"], w=["bst_v"])
                        S.op("dve", lambda hh: hh.scalar_tensor_tensor(out=v_, in0=q_, scalar=1.0 / 1536, in1=v_,
                                                                       op0=ALU.mult, op1=ALU.subtract),
                             r=["bst_q", "bst_v"], w=["bst_v"])
                        S.op("dve", lambda hh: hh.tensor_scalar(out=v_, in0=v_, scalar1=EPS, scalar2=None,
                                                                op0=ALU.add), r=["bst_v"], w=["bst_v"])
                        S.op("act", lambda hh: hh.activation(out=v_, in_=v_, func=AF.Sqrt), r=["bst_v"], w=["bst_v"])
                        S.op("dve", lambda hh: hh.reciprocal(out=r_, in_=v_), r=["bst_v"], w=["bst_r"])
                        S.op("dve", lambda hh: hh.scalar_tensor_tensor(out=n_, in0=m_, scalar=-1.0, in1=r_,
                                                                       op0=ALU.mult, op1=ALU.mult),
                             r=["bst_m", "bst_r"], w=["bst_n"])
                        for tb in range(8):
                            S.op("dve", lambda hh, tb=tb: hh.tensor_scalar(
                                out=gbuf[:, tb, :], in0=gbuf[:, tb, :], scalar1=SM[:, c_bst + 24 + tb:c_bst + 25 + tb],
                                scalar2=SM[:, c_bst + 32 + tb:c_bst + 33 + tb], op0=ALU.mult, op1=ALU.add),
                                r=["bst_r", "bst_n"] + arc(GB_OFF + tb * 3072, 3072),
                                w=arc(GB_OFF + tb * 3072, 3072))
                    n_ug = [0]
                    for cc in range(3):
                        g = 3 * k + cc
                        for s, (t0, hc) in enumerate(CORE_SLABS):
                            b = S.banks()
                            mm(psb(b, 512), [(W[slot][:, kc, cc * 128:(cc + 1) * 128], hT[:, kc, t0:t0 + 512])
                                             for kc in range(16)], r=WC[slot] + hc, w=["P%d" % b])
                            ui = n_ug[0] % 2
                            n_ug[0] += 1
                            ug = arv(UG_OFF[ui], 512, BF16)
                            ugc = arc(UG_OFF[ui], 1024)
                            S.op("act", lambda hh, ug=ug, b=b: hh.activation(out=ug, in_=psb(b, 512), func=AF.Gelu),
                                 r=[], w=ugc + ["P%d" % b])
                            ob, obc = ot_slab(12 + g, s)
                            S.op("dve", lambda hh, ug=ug, ob=ob: hh.tensor_tensor(out=ob, in0=ug, in1=ob, op=ALU.mult),
                                 r=ugc + obc, w=obc)
                    for cc in range(3):
                        g = 3 * k + cc
                        for s in range(2):
                            b = S.banks()

                            def emit(hh, g=g, s=s, b=b):
                                ins = None
                                for q in range(4):
                                    tb = 4 * s + q
                                    ins = hh.matmul(psb(b, 128, q * 128), lhsT=gbuf[:, tb, g * 128:(g + 1) * 128],
                                                    rhs=WST[:, g, :], start=True, stop=True)
                                return ins
                            S.op("pe", emit, r=arc(GB_OFF + 4 * s * 3072, 4 * 3072) + ["wst"], w=["P%d" % b])
                            ti = (cc * 2 + s) % 2
                            t1 = arv(T1_OFF[ti], 512, F32)
                            t1c = arc(T1_OFF[ti], 2048)
                            for q in range(4):
                                S.op("dve", lambda hh, t1=t1, b=b, g=g, q=q: hh.scalar_tensor_tensor(
                                    out=t1[:, q * 128:(q + 1) * 128], in0=psb(b, 128, q * 128),
                                    scalar=SM[:, c_lng + g:c_lng + g + 1], in1=CC[:, g, :],
                                    op0=ALU.mult, op1=ALU.add), r=["lng", "cc"], w=t1c + ["P%d" % b])
                            ob, obc = ot_slab(12 + g, s)
                            S.op("dve", lambda hh, t1=t1, ob=ob: hh.tensor_tensor(out=ob, in0=t1, in1=ob, op=ALU.mult),
                                 r=t1c + obc, w=obc)
                return f
            for k in range(4):
                rest.append(([(w_in, C_UB + k * 384, 384)], make_ub(k)))

            mT = arv(0, 16384, BF16).rearrange("p (c t) -> p c t", t=1024)
            GT_OFF = (0, 2048)
            ACC_OFF = 4096
            acc = GN[:, 1024:1536]
            acc_c = gnc(ACC_OFF, 2048)
            OA_ALL = otc(0, 24576)
            OB_ALL = otc(24576, 24576)
            OC_ALL = otc(49152, 16384)

            BRS = ((w_bra, 12, 0, 0, OA_ALL), (w_brb, 12, 12, 12, OB_ALL), (w_brc, 8, 24, 24, OC_ALL))

            def load_wb(cc, i):
                wsrc, nk, k0, _, _ = BRS[i]
                S.op("pool", lambda hh: hh.dma_start(
                    out=WB[:, k0:k0 + nk, :],
                    in_=wsrc[:, cc * 128:(cc + 1) * 128].rearrange("(kc p) n -> p kc n", p=128)),
                    w=["WB%d" % i], dma=True)

            def make_merge(cc):
                def f(slot):
                    if cc == 0:
                        for i in range(3):
                            load_wb(0, i)
                    gts = [GN[:, 0:512], GN[:, 512:1024]]
                    gtcs = [gnc(0, 2048), gnc(2048, 2048)]
                    accs = [GN[:, 1024:1536], GN[:, 1536:2048]]
                    acccs = [gnc(4096, 2048), gnc(6144, 2048)]
                    for i, (wsrc, nk, k0, och0, ocells) in enumerate(BRS):
                        for s, (t0, hc) in enumerate(CORE_SLABS):
                            bg = S.banks()
                            mm(psb(bg, 512), [(W[slot][:, kc, i * 128:(i + 1) * 128], hT[:, kc, t0:t0 + 512])
                                              for kc in range(16)], r=WC[slot] + hc, w=["P%d" % bg])
                            S.op("act", lambda hh, s=s, bg=bg: hh.activation(out=gts[s], in_=psb(bg, 512),
                                                                             func=AF.Sigmoid),
                                 r=[], w=gtcs[s] + ["P%d" % bg])
                        for s in range(2):
                            gt, gtc, acc, acc_c = gts[s], gtcs[s], accs[s], acccs[s]
                            bm = S.banks()
                            mm(psb(bm, 512), [(WB[:, k0 + kc, :], OT[:, (och0 + kc) * 1024 + s * 512:
                                                                     (och0 + kc) * 1024 + (s + 1) * 512])
                                              for kc in range(nk)], r=["WB%d" % i] + ocells, w=["P%d" % bm])
                            if i == 0:
                                S.op("dve", lambda hh, gt=gt, bm=bm, acc=acc: hh.tensor_tensor(
                                    out=acc, in0=psb(bm, 512), in1=gt, op=ALU.mult),
                                    r=gtc, w=acc_c + ["P%d" % bm])
                            else:
                                S.op("dve", lambda hh, gt=gt, bm=bm: hh.tensor_tensor(
                                    out=gt, in0=psb(bm, 512), in1=gt, op=ALU.mult),
                                    r=gtc, w=gtc + ["P%d" % bm])
                                if i == 1:
                                    S.op("dve", lambda hh, gt=gt, acc=acc: hh.tensor_tensor(out=acc, in0=acc, in1=gt,
                                                                                            op=ALU.add),
                                         r=gtc + acc_c, w=acc_c)
                                else:
                                    S.op("dve", lambda hh, gt=gt, acc=acc, s=s: hh.tensor_tensor(
                                        out=mT[:, cc, s * 512:(s + 1) * 512], in0=acc, in1=gt, op=ALU.add),
                                        r=gtc + acc_c, w=arc(cc * 2048 + s * 1024, 1024))
                        if cc + 1 < 16:
                            load_wb(cc + 1, i)
                return f
            for cc in range(16):
                rest.append(([(w_in, C_G + cc * 128, 128), (w_in, C_G + 2048 + cc * 128, 128),
                                (w_in, C_G + 4096 + cc * 128, 128)], make_merge(cc)))

            MT_ALL = arc(0, 32768)
            JK_OFF = 32768
            jk = arv(JK_OFF, 2048, BF16)
            jk_c = arc(JK_OFF, 4096)
            c_fst = smalloc(6)
            c_fq = smalloc(48)

            def xo_view(tb):
                return OT[:, tb * 4096:(tb + 1) * 4096].bitcast(F32)

            def make_out(c0, ncol, first, last, bi):
                def f(slot):
                    if first:
                        load_gain(fg_d)
                        for tb in range(8):
                            S.op("sp", lambda hh, tb=tb: hh.dma_start(
                                out=xo_view(tb), in_=xp[r0 + (tb + 1) * 128:r0 + (tb + 2) * 128, :]),
                                w=otc(tb * 8192, 8192), dma=True)
                    for tb in range(8):
                        b = S.banks()
                        mm(psb(b, ncol), [(mT[:, kc, tb * 128:(tb + 1) * 128], W[slot][:, kc, 0:ncol])
                                          for kc in range(16)], r=WC[slot] + MT_ALL, w=["P%d" % b])
                        xs_ = xo_view(tb)[:, c0:c0 + ncol]
                        xc_ = otc(tb * 8192 + c0 * 4, ncol * 4)
                        S.op("dve", lambda hh, xs_=xs_, b=b: hh.tensor_tensor(out=xs_, in0=psb(b, ncol), in1=xs_,
                                                                             op=ALU.add),
                             r=xc_, w=xc_ + ["P%d" % b])
                        fqc = "fq%d_%d" % (tb, bi)
                        S.op("act", lambda hh, xs_=xs_, tb=tb: hh.activation(
                            out=jk[:, 0:ncol], in_=xs_, func=AF.Square,
                            accum_out=SM[:, c_fq + tb * 6 + bi:c_fq + tb * 6 + bi + 1]),
                            r=xc_, w=jk_c + [fqc])
                        if last:
                            i = tb % 2
                            c = c_fst + 3 * i
                            stc = "fst%d" % i
                            xo = xo_view(tb)
                            xoc = otc(tb * 8192, 8192)
                            S.op("dve", lambda hh, c=c, tb=tb: hh.tensor_reduce(
                                out=SM[:, c:c + 1], in_=SM[:, c_fq + tb * 6:c_fq + tb * 6 + 6], axis=AX.X,
                                op=ALU.add), r=["fq%d_%d" % (tb, q) for q in range(6)], w=[stc])
                            S.op("act", lambda hh, c=c: hh.activation(out=SM[:, c + 1:c + 2], in_=SM[:, c:c + 1],
                                                                      func=AF.Sqrt, scale=1.0 / D,
                                                                      bias=SM[:, c_eps:c_eps + 1]),
                                 r=[stc, "eps"], w=[stc])
                            S.op("dve", lambda hh, c=c: hh.reciprocal(out=SM[:, c + 2:c + 3], in_=SM[:, c + 1:c + 2]),
                                 r=[stc], w=[stc])
                            S.op("dve", lambda hh, xo=xo, c=c: hh.scalar_tensor_tensor(
                                out=xo, in0=xo, scalar=SM[:, c + 2:c + 3], in1=GN[:], op0=ALU.mult, op1=ALU.mult),
                                r=xoc + [stc] + GN_ALL, w=xoc)
                            out_toks.append(S.op("sp", lambda hh, xo=xo, tb=tb: hh.dma_start(
                                out=y[r0 + tb * 128:r0 + (tb + 1) * 128, :], in_=xo), r=xoc, dma=True))
                return f
            blocks = [(1920, 128), (0, 384), (384, 384), (768, 384), (1152, 384), (1536, 384)]
            for bi, (c0, ncol) in enumerate(blocks):
                rest.append(([(w_out, c0, ncol)], make_out(c0, ncol, bi == 0, bi == len(blocks) - 1, bi)))
            return p0l, zal, zcl, zbl, rest

        out_toks = []
        for tt in range(2):
            p0l, zal, zcl, zbl, rest = tile_stages(tt)
            stages.extend(p0l)
            if tt == 0:
                stages.append(zal[0])
                stages.append((None, mem_prep))
                stages.extend(zal[1:] + zbl + zcl)
                for i, stg in enumerate(rest):
                    stages.append(stg)
                    k = i - 4
                    if 0 <= k < len(kvst):
                        stages.append(kvst[k])
            else:
                stages.extend(zal + zbl + zcl)
                stages.extend(rest)

        widx = [i for i, (segs, _) in enumerate(stages) if segs is not None]
        wpos = {si: k for k, si in enumerate(widx)}
        if widx:
            load_w(0, stages[widx[0]][0])
        for si, (segs, fn) in enumerate(stages):
            if segs is None:
                fn(None)
                continue
            k = wpos[si]
            if k + 1 < len(widx):
                load_w((k + 1) % 2, stages[widx[k + 1]][0])
            fn(k % 2)
        S.wait_all("sp", out_toks)
        S.finish(st)
    return nc


_CACHE = {}


def _consts():
    nd = np.zeros((128, 384), np.float32)
    b = np.arange(128)[:, None].astype(np.float32)
    a = np.arange(128)[None, :].astype(np.float32)
    d_m1 = 128 + b - a
    nd[:, 0:128] = np.where(d_m1 <= 128, -d_m1, BIGNEG)
    nd[:, 128:256] = -np.abs(a - b)
    d_p1 = 128 + a - b
    nd[:, 256:384] = np.where(d_p1 <= 128, -d_p1, BIGNEG)
    ident = np.eye(128, dtype=np.float32)
    return nd.astype(ml_dtypes.bfloat16), ident.astype(ml_dtypes.bfloat16)


def kernel(x, mem, norm_gain, mem_norm_gain, w_in, sink, ln_v_gain, ln_v_bias, w_spatial, b_spatial,
           w_kv_mem, w_br_a, w_br_b, w_br_c, w_out, final_gain):
    f32 = np.float32
    x = np.asarray(x, f32)
    mem = np.asarray(mem, f32)
    B, SEQ, _ = x.shape
    if "nc" not in _CACHE:
        _CACHE["nc"] = build_program()
    nc = _CACHE["nc"]
    nd, ident = _consts()
    w_in2 = np.ascontiguousarray(np.asarray(w_in, f32)[0])
    w_kv2 = np.ascontiguousarray(np.asarray(w_kv_mem, f32)[0])
    w_bra2 = np.ascontiguousarray(np.asarray(w_br_a, f32)[0])
    w_brb2 = np.ascontiguousarray(np.asarray(w_br_b, f32)[0])
    w_brc2 = np.ascontiguousarray(np.asarray(w_br_c, f32)[0])
    w_out2 = np.ascontiguousarray(np.asarray(w_out, f32)[0])
    ng = np.asarray(norm_gain, f32).reshape(1, D)
    mg = np.asarray(mem_norm_gain, f32).reshape(1, D)
    fg = np.asarray(final_gain, f32).reshape(1, D)
    sk = np.asarray(sink, f32).reshape(1, 12)
    lng = np.ascontiguousarray(np.asarray(ln_v_gain, f32).reshape(12, 128).T)
    lnb = np.ascontiguousarray(np.asarray(ln_v_bias, f32).reshape(12, 128).T)
    wsT = np.ascontiguousarray(np.asarray(w_spatial, f32)[0].transpose(2, 0, 1).reshape(128, 1536))
    bs = np.ascontiguousarray(np.asarray(b_spatial, f32)[0].reshape(1, 1536))
    zpad = np.zeros((128, D), f32)
    in_maps = []
    for c in range(NCORES):
        b, half = c // 2, c % 2
        xb = x[b]
        lo = half * 2048 - 128
        hi = half * 2048 + 2048 + 128
        parts = []
        parts.append(zpad if lo < 0 else xb[lo:lo + 128])
        parts.append(xb[half * 2048: half * 2048 + 2048])
        parts.append(zpad if hi > SEQ else xb[hi - 128:hi])
        xpc = np.ascontiguousarray(np.concatenate(parts, axis=0))
        edge = np.zeros((128, 4), f32)
        if half == 0:
            edge[:, 0] = EDGE_NEG
        if half == 1:
            edge[:, 3] = EDGE_NEG
        in_maps.append(dict(xp=xpc, mem=np.ascontiguousarray(mem[b]), w_in=w_in2, w_kv=w_kv2, w_bra=w_bra2,
                            w_brb=w_brb2, w_brc=w_brc2, w_out=w_out2, ng=ng, mg=mg, fg=fg, sink=sk, lng=lng,
                            lnb=lnb, wsT=wsT, bs=bs, edge=edge, nd=nd, ident=ident))
    res = run_bass_kernel_spmd(nc, in_maps, core_ids=list(range(NCORES)))
    out = np.empty((B, SEQ, D), f32)
    for c in range(NCORES):
        b, half = c // 2, c % 2
        out[b, half * 2048:(half + 1) * 2048] = np.asarray(res.results[c]["y"], f32)
    return out
```

```python
import math
from contextlib import ExitStack

import numpy as np
import ml_dtypes

import concourse.bass as bass
import concourse.mybir as mybir
from concourse.bass_utils import run_bass_kernel_spmd

F32 = mybir.dt.float32
BF16 = mybir.dt.bfloat16
AF = mybir.ActivationFunctionType
ALU = mybir.AluOpType
AX = mybir.AxisListType

D = 2048
NCORES = 8
TT = 1024
NTB = 10
EPS = 1e-6
SCALE_A = 1.0 / math.sqrt(128.0)
SCALE_C = 1.0 / math.sqrt(256.0)
BIGNEG = -1.0e6
EDGE_NEG = -30000.0
C_Q, C_K, C_V, C_ZA = 0, 1536, 2048, 2560
C_UB, C_VB, C_ZB = 4096, 5632, 7168
C_QC, C_ZC, C_G = 8704, 9728, 10752

ENGS = ["pe", "act", "dve", "pool", "sp"]


def alibi_slopes(n):
    def pow2_slopes(m):
        start = 2.0 ** (-8.0 / m)
        return [start ** (i + 1) for i in range(m)]
    if math.log2(n).is_integer():
        s = pow2_slopes(n)
    else:
        c = 2 ** int(math.floor(math.log2(n)))
        s = pow2_slopes(c) + pow2_slopes(2 * c)[0::2][: n - c]
    return [float(v) for v in s]


class Sched:
    RING = 8

    def __init__(self, nc):
        self.nc = nc
        self.ops = {e: [] for e in ENGS}
        self.cw = {}
        self.cr = {}
        self.ndma = {e: 0 for e in ENGS}
        self.bank = 0

    def banks(self, n=1):
        if n == 2 and self.bank % 2:
            self.bank += 1
        b = self.bank % 8
        self.bank = (self.bank + n) % 8
        return b

    def op(self, eng, emit, r=(), w=(), dma=False):
        ops = self.ops[eng]
        tok = (eng, len(ops))
        deps = set()
        raw = set()
        for c in r:
            t = self.cw.get(c)
            if t is not None:
                deps.add(t)
                raw.add(t)
        for c in w:
            t = self.cw.get(c)
            if t is not None:
                deps.add(t)
            for t in self.cr.get(c, ()):
                deps.add(t)
        fdeps = set()
        for t in deps:
            if t[0] == eng and eng == "pe":
                continue
            fdeps.add(t)
        rec = dict(emit=emit, deps=fdeps, dma=dma, sig=dma, dn=None)
        if dma:
            n = self.ndma[eng]
            rec["dn"] = n
            self.ndma[eng] += 1
            if n >= self.RING:
                for j in range(len(ops) - 1, -1, -1):
                    if ops[j]["dma"] and ops[j]["dn"] == n - self.RING:
                        fdeps.add((eng, j))
                        break
        ops.append(rec)
        ws = set(w)
        for c in w:
            self.cw[c] = tok
            self.cr[c] = []
        for c in r:
            if c not in ws:
                self.cr.setdefault(c, []).append(tok)
        return tok

    def wait_all(self, eng, toks):
        self.ops[eng].append(dict(emit=None, deps=set(toks), dma=False, sig=False, dn=None))

    def finish(self, stack):
        nc = self.nc
        for e in ENGS:
            for rec in self.ops[e]:
                for (de, di) in rec["deps"]:
                    self.ops[de][di]["sig"] = True
        prog = {e: stack.enter_context(nc.semaphore("prog_" + e)) for e in ENGS}
        rings = {}
        for e in ENGS:
            if self.ndma[e]:
                rings[e] = [stack.enter_context(nc.semaphore("ring_%s_%d" % (e, i)))
                            for i in range(min(self.RING, self.ndma[e]))]
        for e in ENGS:
            cnt = 0
            for rec in self.ops[e]:
                if rec["dma"]:
                    n = rec["dn"]
                    rec["sv"] = (rings[e][n % self.RING], 16 * (n // self.RING + 1))
                elif rec["sig"]:
                    cnt += 1
                    rec["sv"] = (prog[e], cnt)
        block = stack.enter_context(nc.Block())

        def run(e):
            def body(h):
                known = {}
                for rec in self.ops[e]:
                    need = {}
                    for (de, di) in rec["deps"]:
                        sem, val = self.ops[de][di]["sv"]
                        k = id(sem)
                        if known.get(k, 0) >= val:
                            continue
                        if k not in need or need[k][1] < val:
                            need[k] = (sem, val)
                    for k, (sem, val) in need.items():
                        h.wait_ge(sem, val)
                        known[k] = val
                    if rec["emit"] is None:
                        continue
                    ins = rec["emit"](h)
                    if rec["dma"]:
                        ins.then_inc(rec["sv"][0], 16)
                    elif rec["sig"]:
                        ins.then_inc(rec["sv"][0], 1)
            return body

        block.tensor(run("pe"))
        block.scalar(run("act"))
        block.vector(run("dve"))
        block.gpsimd(run("pool"))
        block.sync(run("sp"))


def build_program():
    nc = bass.Bass("TRN2", target_bir_lowering=False)

    def din(name, shape, dt=F32):
        return nc.dram_tensor(name, list(shape), dt, kind="ExternalInput").ap()

    xp = din("xp", [2304, D])
    memd = din("mem", [256, D])
    w_in = din("w_in", [D, 16896])
    w_kv = din("w_kv", [D, 2048])
    w_bra = din("w_bra", [1536, D])
    w_brb = din("w_brb", [1536, D])
    w_brc = din("w_brc", [1024, D])
    w_out = din("w_out", [D, D])
    ng_d = din("ng", [1, D])
    mg_d = din("mg", [1, D])
    fg_d = din("fg", [1, D])
    sink_d = din("sink", [1, 12])
    lng_d = din("lng", [128, 12])
    lnb_d = din("lnb", [128, 12])
    wsT_d = din("wsT", [128, 1536])
    bs_d = din("bs", [1, 1536])
    edge_d = din("edge", [128, 4])
    nd_d = din("nd", [128, 384], BF16)
    ident_d = din("ident", [128, 128], BF16)
    y = nc.dram_tensor("y", [2048, D], F32, kind="ExternalOutput").ap()

    slopes = alibi_slopes(12)

    with ExitStack() as st:
        def sb(name, shape, dt):
            return st.enter_context(nc.sbuf_tensor(name, list(shape), dt))

        hT = sb("hT", [128, 16, NTB * 128], BF16)
        OT = sb("OT", [128, 32768], BF16)
        AR = sb("AR", [128, 18432], BF16)
        KcT = sb("KcT", [128, 8, 256], BF16)
        Vc = sb("Vc", [128, 2, 1024], BF16)
        W = [sb("W0", [128, 16, 384], BF16), sb("W1", [128, 16, 384], BF16)]
        WB = sb("WB", [128, 32, 128], BF16)
        GN = sb("GN", [128, 2048], F32)
        ND = sb("ND", [128, 384], BF16)
        IDT = sb("IDT", [128, 128], BF16)
        ONES = sb("ONES", [128, 128], BF16)
        WST = sb("WST", [128, 12, 128], BF16)
        CC = sb("CC", [128, 12, 128], F32)
        SM = sb("SM", [128, 512], F32)
        ps = st.enter_context(nc.psum_tensor("ps", [128, 4096], F32))

        S = Sched(nc)

        def arv(off, cols, dt, extra=None):
            nb = cols * (4 if dt == F32 else 2)
            v = AR[:, off // 2:(off + nb) // 2]
            if dt == F32:
                v = v.bitcast(F32)
            return v

        def arc(off, nbytes):
            return ["AR%d" % i for i in range(off // 1024, (off + nbytes + 1023) // 1024)]

        def gnc(off, nbytes):
            return ["GN%d" % i for i in range(off // 1024, (off + nbytes + 1023) // 1024)]

        def otc(off, nbytes):
            return ["OT%d" % i for i in range(off // 1024, (off + nbytes + 1023) // 1024)]

        GN_ALL = gnc(0, 8192)

        def ot_slab(chunk, s):
            e0 = chunk * 1024 + s * 512
            return OT[:, e0:e0 + 512], otc(e0 * 2, 1024)

        def psb(bank, n, off=0):
            return ps[:, bank * 512 + off: bank * 512 + off + n]

        smn = [0]

        def smalloc(n):
            c0 = smn[0]
            smn[0] += n
            assert smn[0] <= 512
            return c0

        WC = [["W0.0", "W0.1", "W0.2"], ["W1.0", "W1.1", "W1.2"]]

        def load_w(slot, segs):
            off = 0
            for i, (src, c0, ncol) in enumerate(segs):
                dst = W[slot][:, :, off:off + ncol]
                srcv = src[:, c0:c0 + ncol].rearrange("(kc p) n -> p kc n", p=128)
                S.op("pool", lambda h, d=dst, s_=srcv: h.dma_start(out=d, in_=s_),
                     w=[WC[slot][i]], dma=True)
                off += ncol

        def mm(out_ap, pairs, r, w):
            def emit(h):
                n = len(pairs)
                ins = None
                for i, (l, rh) in enumerate(pairs):
                    ins = h.matmul(out_ap, lhsT=l, rhs=rh, start=(i == 0), stop=(i == n - 1))
                return ins
            return S.op("pe", emit, r=r, w=w)

        cpy_rr = [0]

        def copy_out(dst, src, r, w, eng=None):
            if eng is None:
                eng = "act" if cpy_rr[0] % 2 == 0 else "dve"
                cpy_rr[0] += 1
            if eng == "act":
                S.op("act", lambda h: h.activation(out=dst, in_=src, func=AF.Copy), r=r, w=w)
            else:
                S.op("dve", lambda h: h.tensor_copy(out=dst, in_=src), r=r, w=w)

        c_eps = smalloc(1)
        c_es = smalloc(12)
        c_sk = smalloc(12)
        c_lng = smalloc(12)
        c_lnb = smalloc(12)
        c_edge = smalloc(4)
        S.op("dve", lambda h: h.memset(SM[:, c_eps:c_eps + 1], EPS), w=["eps"])
        S.op("dve", lambda h: h.memset(ONES[:], 1.0), w=["ones"])
        S.op("sp", lambda h: h.dma_start(out=IDT[:], in_=ident_d[:, :]), w=["idt"], dma=True)
        S.op("sp", lambda h: h.dma_start(out=ND[:], in_=nd_d[:, :]), w=["nd"], dma=True)
        S.op("sp", lambda h: h.dma_start(out=SM[:, c_edge:c_edge + 4], in_=edge_d[:, :]), w=["edge"], dma=True)
        S.op("sp", lambda h: h.dma_start(out=SM[:, c_lng:c_lng + 12], in_=lng_d[:, :]), w=["lng"], dma=True)
        S.op("sp", lambda h: h.dma_start(out=SM[:, c_lnb:c_lnb + 12], in_=lnb_d[:, :]), w=["lnb"], dma=True)
        S.op("sp", lambda h: h.dma_start(out=SM[:, c_sk:c_sk + 12],
                                         in_=sink_d[0:1, :].broadcast_to([128, 12])), w=["sk"], dma=True)
        S.op("sp", lambda h: h.dma_start(out=CC[:].rearrange("p g t -> p (g t)"),
                                         in_=bs_d[0:1, :].broadcast_to([128, 1536])), w=["cc"], dma=True)
        S.op("pool", lambda h: h.dma_start(out=WST[:].rearrange("p g t -> p (g t)"), in_=wsT_d[:, :]),
             w=["wst"], dma=True)
        S.op("act", lambda h: h.activation(out=SM[:, c_es:c_es + 12], in_=SM[:, c_sk:c_sk + 12], func=AF.Exp),
             r=["sk"], w=["es"])
        for k in range(3):
            b = S.banks()
            mm(psb(b, 512), [(ONES[:], WST[:].rearrange("p g t -> p (g t)")[:, k * 512:(k + 1) * 512])],
               r=["ones", "wst"], w=["P%d" % b])
            for gg in range(4):
                g = k * 4 + gg
                S.op("dve", lambda h, g=g, gg=gg, b=b: h.scalar_tensor_tensor(
                    out=CC[:, g, :], in0=psb(b, 128, gg * 128), scalar=SM[:, c_lnb + g:c_lnb + g + 1],
                    in1=CC[:, g, :], op0=ALU.mult, op1=ALU.add),
                    r=["lnb"], w=["cc", "P%d" % b])

        XS_OFF = [0, 8192, 16384]
        HN_OFF = [24576, 28672, 32768]
        rms_n = [0]

        def rms_block(src_rows, dstT, dst_cells):
            i = rms_n[0] % 3
            rms_n[0] += 1
            xs = arv(XS_OFF[i], 2048, F32)
            hn = arv(HN_OFF[i], 2048, BF16)
            xsc = arc(XS_OFF[i], 8192)
            hnc = arc(HN_OFF[i], 4096)
            if i not in rms_cols:
                rms_cols[i] = smalloc(3)
            c = rms_cols[i]
            stc = "rst%d" % i
            S.op("sp", lambda h: h.dma_start(out=xs, in_=src_rows), w=xsc, dma=True)
            S.op("act", lambda h: h.activation(out=hn, in_=xs, func=AF.Square, accum_out=SM[:, c:c + 1]),
                 r=xsc, w=hnc + [stc])
            S.op("act", lambda h: h.activation(out=SM[:, c + 1:c + 2], in_=SM[:, c:c + 1], func=AF.Sqrt,
                                               scale=1.0 / D, bias=SM[:, c_eps:c_eps + 1]),
                 r=[stc, "eps"], w=[stc])
            S.op("dve", lambda h: h.reciprocal(out=SM[:, c + 2:c + 3], in_=SM[:, c + 1:c + 2]), r=[stc], w=[stc])
            S.op("dve", lambda h: h.scalar_tensor_tensor(out=hn, in0=xs, scalar=SM[:, c + 2:c + 3], in1=GN[:],
                                                         op0=ALU.mult, op1=ALU.mult),
                 r=xsc + [stc] + GN_ALL, w=hnc)
            b = S.banks(2)
            tp = ps[:, b * 512:b * 512 + 1024].bitcast(BF16)

            def tr(h):
                ins = None
                for kc in range(16):
                    ins = h.transpose(out=tp[:, kc * 128:(kc + 1) * 128], in_=hn[:, kc * 128:(kc + 1) * 128],
                                      identity=IDT[:])
                return ins
            pc = ["P%d" % b, "P%d" % (b + 1)]
            S.op("pe", tr, r=hnc + ["idt"], w=pc)

            def fin():
                copy_out(dstT, tp.rearrange("p (k t) -> p k t", t=128), r=[], w=dst_cells + pc)
            return fin

        rms_cols = {}

        def load_gain(src):
            S.op("sp", lambda h: h.dma_start(out=GN[:], in_=src[0:1, :].broadcast_to([128, D])),
                 w=GN_ALL, dma=True)

        stages = []

        memT = WB[:].rearrange("p a b -> p (a b)").rearrange("p (k t) -> p k t", t=256)
        memT_c = ["WB0", "WB1", "WB2"]

        def mem_prep(slot):
            load_gain(mg_d)
            pend = None
            for mb in range(2):
                f = rms_block(memd[mb * 128:(mb + 1) * 128, :], memT[:, :, mb * 128:(mb + 1) * 128], memT_c)
                if pend is not None:
                    pend()
                pend = f
            pend()

        def make_kvK(c0, ncol):
            def f(slot):
                for cc in range(ncol // 128):
                    b = S.banks()
                    mm(psb(b, 256), [(W[slot][:, kc, cc * 128:(cc + 1) * 128], memT[:, kc, :]) for kc in range(16)],
                       r=WC[slot] + memT_c, w=["P%d" % b])
                    ch = (c0 + cc * 128) // 128
                    copy_out(KcT[:, ch, :], psb(b, 256), r=[], w=["kct%d" % ch, "P%d" % b])
            return f

        def make_kvV(c0, ncol):
            def f(slot):
                for mb in range(2):
                    b = S.banks()
                    mm(psb(b, ncol), [(memT[:, kc, mb * 128:(mb + 1) * 128], W[slot][:, kc, 0:ncol])
                                      for kc in range(16)],
                       r=WC[slot] + memT_c, w=["P%d" % b])
                    copy_out(Vc[:, mb, c0 - 1024:c0 - 1024 + ncol], psb(b, ncol), r=[],
                             w=["vc%d_%d" % (mb, c0), "P%d" % b])
            return f
        VC_CELLS = ["vc%d_%d" % (mb, c0) for mb in range(2) for c0 in (1024, 1408, 1792)]
        KCT_CELLS = ["kct%d" % i for i in range(8)]
        kvst = []
        for (c0, ncol) in ((0, 384), (384, 384), (768, 256)):
            kvst.append(([(w_kv, c0, ncol)], make_kvK(c0, ncol)))
        for (c0, ncol) in ((1024, 384), (1408, 384), (1792, 256)):
            kvst.append(([(w_kv, c0, ncol)], make_kvV(c0, ncol)))

        HT_CELLS = ["hT%d" % i for i in range(NTB)]
        CORE_SLABS = [(128, ["hT1", "hT2", "hT3", "hT4"]), (640, ["hT5", "hT6", "hT7", "hT8"])]

        V_OFF, KT_OFF, QT_OFF, E_OFF, TMP_OFF = 0, 10240, 20480, 24576, (30720, 32768)
        Vt = arv(V_OFF, 5120, BF16).rearrange("p (b c) -> p b c", c=512)
        kTt = arv(KT_OFF, 5120, BF16).rearrange("p (g t) -> p g t", t=1280)
        RR_OFF = 0
        rr = GN[:, 0:512]
        rr_c = gnc(0, 2048)
        QB0 = [max(1, j - 1) for j in range(NTB)]
        QB1 = [min(8, j + 1) for j in range(NTB)]
        NJ = [(QB1[j] - QB0[j] + 1) * 128 for j in range(NTB)]
        EO = [sum(NJ[:j]) for j in range(NTB)]

        def tile_stages(tt):
            r0 = tt * 1024
            p0l, zal, zcl, zbl, rest = [], [], [], [], []

            def phase0(slot):
                load_gain(ng_d)
                pend = None
                for tb in range(NTB):
                    f = rms_block(xp[r0 + tb * 128:r0 + (tb + 1) * 128, :], hT[:, :, tb * 128:(tb + 1) * 128],
                                  ["hT%d" % tb])
                    if pend is not None:
                        pend()
                    pend = f
                pend()
            p0l.append((None, phase0))

            def make_z(chunk0, nch):
                def f(slot):
                    for cc in range(nch):
                        for s, (t0, hc) in enumerate(CORE_SLABS):
                            b = S.banks()
                            mm(psb(b, 512), [(W[slot][:, kc, cc * 128:(cc + 1) * 128], hT[:, kc, t0:t0 + 512])
                                             for kc in range(16)], r=WC[slot] + hc, w=["P%d" % b])
                            dst, dc = ot_slab(chunk0 + cc, s)
                            S.op("act", lambda h, dst=dst, b=b: h.activation(out=dst, in_=psb(b, 512), func=AF.Silu),
                                 r=[], w=dc + ["P%d" % b])
                return f
            for k in range(4):
                zal.append(([(w_in, C_ZA + k * 384, 384)], make_z(3 * k, 3)))
            for (c0, ncol) in ((0, 384), (384, 384), (768, 256)):
                zcl.append(([(w_in, C_ZC + c0, ncol)], make_z(24 + c0 // 128, ncol // 128)))
            for k in range(4):
                zbl.append(([(w_in, C_ZB + k * 384, 384)], make_z(12 + 3 * k, 3)))

            def make_v(c0, ncol):
                def f(slot):
                    for tb in range(NTB):
                        b = S.banks()
                        mm(psb(b, ncol), [(hT[:, kc, tb * 128:(tb + 1) * 128], W[slot][:, kc, 0:ncol])
                                          for kc in range(16)], r=WC[slot] + ["hT%d" % tb], w=["P%d" % b])
                        copy_out(Vt[:, tb, c0:c0 + ncol], psb(b, ncol), r=[],
                                 w=arc(V_OFF + tb * 1024 + c0 * 2, ncol * 2) + ["P%d" % b])
                return f
            rest.append(([(w_in, C_V, 384)], make_v(0, 384)))
            rest.append(([(w_in, C_V + 384, 128)], make_v(384, 128)))

            def make_k(g0, ng):
                def f(slot):
                    for gg in range(ng):
                        g = g0 + gg
                        for (t0, n, hc) in ((0, 512, HT_CELLS[0:4]), (512, 512, HT_CELLS[4:8]),
                                            (1024, 256, HT_CELLS[8:10])):
                            b = S.banks()
                            mm(psb(b, n), [(W[slot][:, kc, gg * 128:(gg + 1) * 128], hT[:, kc, t0:t0 + n])
                                           for kc in range(16)], r=WC[slot] + hc, w=["P%d" % b])
                            copy_out(kTt[:, g, t0:t0 + n], psb(b, n), r=[],
                                     w=arc(KT_OFF + g * 2560 + t0 * 2, n * 2) + ["P%d" % b])
                return f
            rest.append(([(w_in, C_K, 384)], make_k(0, 3)))
            rest.append(([(w_in, C_K + 384, 128)], make_k(3, 1)))

            V_ALL = arc(V_OFF, 10240)
            KT_ALL = arc(KT_OFF, 10240)
            E_OFFS = (24576, 30720)
            rrs = [GN[:, 0:512], GN[:, 512:1024]]
            rrs_c = [gnc(0, 2048), gnc(2048, 2048)]

            def pv(h):
                g = h // 3
                eoff = E_OFFS[h % 2]
                Et = arv(eoff, 3072, BF16)
                E_ALL = arc(eoff, 6144)
                for s in range(2):
                    bo = S.banks()
                    br = S.banks()

                    def emit(hh, s=s, bo=bo, br=br, g=g, Et=Et):
                        ins = None
                        for qi in range(4):
                            i = 1 + 4 * s + qi
                            js = [i - 1, i, i + 1]
                            for idx, j in enumerate(js):
                                e0 = EO[j] + (i - QB0[j]) * 128
                                ins = hh.matmul(psb(bo, 128, qi * 128), lhsT=Vt[:, j, g * 128:(g + 1) * 128],
                                                rhs=Et[:, e0:e0 + 128], start=(idx == 0), stop=(idx == 2))
                        for qi in range(4):
                            i = 1 + 4 * s + qi
                            js = [i - 1, i, i + 1]
                            for idx, j in enumerate(js):
                                e0 = EO[j] + (i - QB0[j]) * 128
                                ins = hh.matmul(psb(br, 128, qi * 128), lhsT=ONES[:],
                                                rhs=Et[:, e0:e0 + 128], start=(idx == 0), stop=(idx == 2))
                        return ins
                    S.op("pe", emit, r=V_ALL + E_ALL + ["ones"], w=["P%d" % bo, "P%d" % br])
                    oa, oac = ot_slab(h, s)
                    rr_, rrc_ = rrs[s], rrs_c[s]
                    S.op("act", lambda hh, br=br, h=h, rr_=rr_: hh.activation(
                        out=rr_, in_=psb(br, 512), func=AF.Identity, bias=SM[:, c_es + h:c_es + h + 1]),
                        r=["es"], w=rrc_ + ["P%d" % br])
                    S.op("dve", lambda hh, rr_=rr_: hh.reciprocal(out=rr_, in_=rr_), r=rrc_, w=rrc_)
                    S.op("dve", lambda hh, oa=oa, rr_=rr_: hh.tensor_tensor(out=rr_, in0=rr_, in1=oa, op=ALU.mult),
                         r=rrc_ + oac, w=rrc_)
                    S.op("dve", lambda hh, oa=oa, bo=bo, rr_=rr_: hh.tensor_tensor(out=oa, in0=psb(bo, 512), in1=rr_,
                                                                                   op=ALU.mult),
                         r=rrc_, w=oac + ["P%d" % bo])

            def make_head(h):
                def f(slot):
                    g = h // 3
                    qoff = QT_OFF + (h % 2) * 2048
                    qT = arv(qoff, 1024, BF16)
                    eoff = E_OFFS[h % 2]
                    Et = arv(eoff, 3072, BF16)
                    qs = SCALE_A / slopes[h]
                    for s, (t0, hc) in enumerate(CORE_SLABS):
                        b = S.banks()
                        mm(psb(b, 512), [(W[slot][:, kc, 0:128], hT[:, kc, t0:t0 + 512]) for kc in range(16)],
                           r=WC[slot] + hc, w=["P%d" % b])
                        S.op("act", lambda hh, b=b, s=s: hh.activation(out=qT[:, s * 512:(s + 1) * 512],
                                                                       in_=psb(b, 512), func=AF.Identity, scale=qs),
                             r=[], w=arc(qoff + s * 1024, 1024) + ["P%d" % b])
                    qc_all = arc(qoff, 2048)
                    for j in range(NTB):
                        b = S.banks()
                        n = NJ[j]
                        q0 = (QB0[j] - 1) * 128
                        nd0 = (QB0[j] - j + 1) * 128
                        mm(psb(b, n), [(kTt[:, g, j * 128:(j + 1) * 128], qT[:, q0:q0 + n]),
                                       (IDT[:], ND[:, nd0:nd0 + n])],
                           r=KT_ALL + qc_all + ["idt", "nd"], w=["P%d" % b])
                        ecells = arc(eoff + EO[j] * 2, n * 2)
                        if j == 0 or j == NTB - 1:
                            ecol = c_edge + tt * 2 + (0 if j == 0 else 1)
                            S.op("act", lambda hh, b=b, j=j, n=n, ecol=ecol: hh.activation(
                                out=Et[:, EO[j]:EO[j] + n], in_=psb(b, n), func=AF.Exp, scale=slopes[h],
                                bias=SM[:, ecol:ecol + 1]), r=["edge"], w=ecells + ["P%d" % b])
                        else:
                            S.op("act", lambda hh, b=b, j=j, n=n: hh.activation(
                                out=Et[:, EO[j]:EO[j] + n], in_=psb(b, n), func=AF.Exp, scale=slopes[h]),
                                r=[], w=ecells + ["P%d" % b])
                    if h > 0:
                        pv(h - 1)
                    if h == 11:
                        pv(11)
                return f
            for h in range(12):
                rest.append(([(w_in, C_Q + h * 128, 128)], make_head(h)))

            QC_OFF, EC_OFF = 0, 4096
            qcT = arv(QC_OFF, 2048, BF16).rearrange("p (d t) -> p d t", t=1024)
            EcT = arv(EC_OFF, 2048, BF16).rearrange("p (m t) -> p m t", t=1024)
            rr2 = GN[:, 512:1024]
            rr2_c = gnc(2048, 2048)

            def make_chead(hc_):
                def f(slot):
                    for dc in range(2):
                        for s, (t0, hc) in enumerate(CORE_SLABS):
                            b = S.banks()
                            mm(psb(b, 512), [(W[slot][:, kc, dc * 128:(dc + 1) * 128], hT[:, kc, t0:t0 + 512])
                                             for kc in range(16)], r=WC[slot] + hc, w=["P%d" % b])
                            copy_out(qcT[:, dc, s * 512:(s + 1) * 512], psb(b, 512), r=[],
                                     w=arc(QC_OFF + dc * 2048 + s * 1024, 1024) + ["P%d" % b])
                    qc_cells = arc(QC_OFF, 4096)
                    for mb in range(2):
                        for s in range(2):
                            b = S.banks()
                            mm(psb(b, 512), [(KcT[:, hc_ * 2 + dc, mb * 128:(mb + 1) * 128],
                                              qcT[:, dc, s * 512:(s + 1) * 512]) for dc in range(2)],
                               r=KCT_CELLS + qc_cells, w=["P%d" % b])
                            S.op("act", lambda hh, b=b, mb=mb, s=s: hh.activation(
                                out=EcT[:, mb, s * 512:(s + 1) * 512], in_=psb(b, 512), func=AF.Exp,
                                scale=SCALE_C), r=[], w=arc(EC_OFF + mb * 2048 + s * 1024, 1024) + ["P%d" % b])
                    ec_cells = arc(EC_OFF, 4096)
                    for s in range(2):
                        br = S.banks()
                        mm(psb(br, 512), [(ONES[:], EcT[:, mb, s * 512:(s + 1) * 512]) for mb in range(2)],
                           r=["ones"] + ec_cells, w=["P%d" % br])
                        S.op("dve", lambda hh, br=br: hh.reciprocal(out=rr, in_=psb(br, 512)), r=[],
                             w=rr_c + ["P%d" % br])
                        for dc in range(2):
                            bo = S.banks()
                            mm(psb(bo, 512), [(Vc[:, mb, hc_ * 256 + dc * 128: hc_ * 256 + (dc + 1) * 128],
                                               EcT[:, mb, s * 512:(s + 1) * 512]) for mb in range(2)],
                               r=VC_CELLS + ec_cells, w=["P%d" % bo])
                            oc, occ = ot_slab(24 + hc_ * 2 + dc, s)
                            S.op("dve", lambda hh, oc=oc: hh.tensor_tensor(out=rr2, in0=rr, in1=oc, op=ALU.mult),
                                 r=rr_c + occ, w=rr2_c)
                            S.op("dve", lambda hh, oc=oc, bo=bo: hh.tensor_tensor(out=oc, in0=psb(bo, 512), in1=rr2,
                                                                                 op=ALU.mult),
                                 r=rr2_c, w=occ + ["P%d" % bo])
                return f
            for hc_ in range(4):
                rest.append(([(w_in, C_QC + hc_ * 256, 256)], make_chead(hc_)))

            GB_OFF = 0
            gbuf = arv(GB_OFF, 12288, BF16).rearrange("p (b c) -> p b c", c=1536)
            UG_OFF = (24576, 25600)
            T1_OFF = (26624, 28672)
            SQ_OFF = 30720
            sqj = arv(SQ_OFF, 512, BF16)
            c_s1 = smalloc(32)
            c_s2 = smalloc(32)
            c_bst = smalloc(48)

            def make_vb(k):
                def f(slot):
                    for tb in range(8):
                        b = S.banks()
                        mm(psb(b, 384), [(hT[:, kc, (tb + 1) * 128:(tb + 2) * 128], W[slot][:, kc, 0:384])
                                         for kc in range(16)], r=WC[slot] + ["hT%d" % (tb + 1)], w=["P%d" % b])
                        gsl = gbuf[:, tb, k * 384:(k + 1) * 384]
                        gcl = arc(GB_OFF + tb * 3072 + k * 768, 768)
                        col = tb * 4 + k
                        S.op("act", lambda hh, gsl=gsl, b=b, col=col: hh.activation(
                            out=gsl, in_=psb(b, 384), func=AF.Gelu, accum_out=SM[:, c_s1 + col:c_s1 + col + 1]),
                            r=[], w=gcl + ["P%d" % b, "s1_%d" % col])
                        S.op("act", lambda hh, gsl=gsl, col=col: hh.activation(
                            out=sqj[:, 0:384], in_=gsl, func=AF.Square, accum_out=SM[:, c_s2 + col:c_s2 + col + 1]),
                            r=gcl, w=arc(SQ_OFF, 1024) + ["s2_%d" % col])
                return f
            for k in range(4):
                rest.append(([(w_in, C_VB + k * 384, 384)], make_vb(k)))

            S1C = ["s1_%d" % i for i in range(32)]
            S2C = ["s2_%d" % i for i in range(32)]

            def make_ub(k):
                def f(slot):
                    if k == 0:
                        m_ = SM[:, c_bst:c_bst + 8]
                        q_ = SM[:, c_bst + 8:c_bst + 16]
                        v_ = SM[:, c_bst + 16:c_bst + 24]
                        r_ = SM[:, c_bst + 24:c_bst + 32]
                        n_ = SM[:, c_bst + 32:c_bst + 40]
                        S.op("dve", lambda hh: hh.tensor_reduce(
                            out=m_, in_=SM[:, c_s1:c_s1 + 32].rearrange("p (b k) -> p b k", k=4), axis=AX.X,
                            op=ALU.add), r=S1C, w=["bst_m"])
                        S.op("dve", lambda hh: hh.tensor_reduce(
                            out=q_, in_=SM[:, c_s2:c_s2 + 32].rearrange("p (b k) -> p b k", k=4), axis=AX.X,
                            op=ALU.add), r=S2C, w=["bst_q"])
                        S.op("dve", lambda hh: hh.tensor_scalar(out=m_, in0=m_, scalar1=1.0 / 1536, scalar2=None,
                                                                op0=ALU.mult), r=["bst_m"], w=["bst_m"])
                        S.op("dve", lambda hh: hh.tensor_tensor(out=v_, in0=m_, in1=m_, op=ALU.mult),
                             r=["bst_m"], w=["bst_v"])
                        S.op("dve", lambda hh: hh.scalar_tensor_tensor(out=v_, in0=q_, scalar=1.0 / 1536, in1=v_,
                                                                       op0=ALU.mult, op1=ALU.subtract),
                             r=["bst_q", "bst_v"], w=["bst_v"])
                        S.op("dve", lambda hh: hh.tensor_scalar(out=v_, in0=v_, scalar1=EPS, scalar2=None,
                                                                op0=ALU.add), r=["bst_v"], w=["bst_v"])
                        S.op("act", lambda hh: hh.activation(out=v_, in_=v_, func=AF.Sqrt), r=["bst_v"], w=["bst_v"])
                        S.op("dve", lambda hh: hh.reciprocal(out=r_, in_=v_), r=["bst_v"], w=["bst_r"])
                        S.op("dve", lambda hh: hh.scalar_tensor_tensor(out=n_, in0=m_, scalar=-1.0, in1=r_,
                                                                       op0=ALU.mult, op1=ALU.mult),
                             r=["bst_m", "bst_r"], w=["bst_n"])
                        for tb in range(8):
                            S.op("dve", lambda hh, tb=tb: hh.tensor_scalar(
                                out=gbuf[:, tb, :], in0=gbuf[:, tb, :], scalar1=SM[:, c_bst + 24 + tb:c_bst + 25 + tb],
                                scalar2=SM[:, c_bst + 32 + tb:c_bst + 33 + tb], op0=ALU.mult, op1=ALU.add),
                                r=["bst_r", "bst_n"] + arc(GB_OFF + tb * 3072, 3072),
                                w=arc(GB_OFF + tb * 3072, 3072))
                    n_ug = [0]
                    for cc in range(3):
                        g = 3 * k + cc
                        for s, (t0, hc) in enumerate(CORE_SLABS):
                            b = S.banks()
                            mm(psb(b, 512), [(W[slot][:, kc, cc * 128:(cc + 1) * 128], hT[:, kc, t0:t0 + 512])
                                             for kc in range(16)], r=WC[slot] + hc, w=["P%d" % b])
                            ui = n_ug[0] % 2
                            n_ug[0] += 1
                            ug = arv(UG_OFF[ui], 512, BF16)
                            ugc = arc(UG_OFF[ui], 1024)
                            S.op("act", lambda hh, ug=ug, b=b: hh.activation(out=ug, in_=psb(b, 512), func=AF.Gelu),
                                 r=[], w=ugc + ["P%d" % b])
                            ob, obc = ot_slab(12 + g, s)
                            S.op("dve", lambda hh, ug=ug, ob=ob: hh.tensor_tensor(out=ob, in0=ug, in1=ob, op=ALU.mult),
                                 r=ugc + obc, w=obc)
                    for cc in range(3):
                        g = 3 * k + cc
                        for s in range(2):
                            b = S.banks()

                            def emit(hh, g=g, s=s, b=b):
                                ins = None
                                for q in range(4):
                                    tb = 4 * s + q
                                    ins = hh.matmul(psb(b, 128, q * 128), lhsT=gbuf[:, tb, g * 128:(g + 1) * 128],
                                                    rhs=WST[:, g, :], start=True, stop=True)
                                return ins
                            S.op("pe", emit, r=arc(GB_OFF + 4 * s * 3072, 4 * 3072) + ["wst"], w=["P%d" % b])
                            ti = (cc * 2 + s) % 2
                            t1 = arv(T1_OFF[ti], 512, F32)
                            t1c = arc(T1_OFF[ti], 2048)
                            for q in range(4):
                                S.op("dve", lambda hh, t1=t1, b=b, g=g, q=q: hh.scalar_tensor_tensor(
                                    out=t1[:, q * 128:(q + 1) * 128], in0=psb(b, 128, q * 128),
                                    scalar=SM[:, c_lng + g:c_lng + g + 1], in1=CC[:, g, :],
                                    op0=ALU.mult, op1=ALU.add), r=["lng", "cc"], w=t1c + ["P%d" % b])
                            ob, obc = ot_slab(12 + g, s)
                            S.op("dve", lambda hh, t1=t1, ob=ob: hh.tensor_tensor(out=ob, in0=t1, in1=ob, op=ALU.mult),
                                 r=t1c + obc, w=obc)
                return f
            for k in range(4):
                rest.append(([(w_in, C_UB + k * 384, 384)], make_ub(k)))

            mT = arv(0, 16384, BF16).rearrange("p (c t) -> p c t", t=1024)
            GT_OFF = (0, 2048)
            ACC_OFF = 4096
            acc = GN[:, 1024:1536]
            acc_c = gnc(ACC_OFF, 2048)
            OA_ALL = otc(0, 24576)
            OB_ALL = otc(24576, 24576)
            OC_ALL = otc(49152, 16384)

            BRS = ((w_bra, 12, 0, 0, OA_ALL), (w_brb, 12, 12, 12, OB_ALL), (w_brc, 8, 24, 24, OC_ALL))

            def load_wb(cc, i):
                wsrc, nk, k0, _, _ = BRS[i]
                S.op("pool", lambda hh: hh.dma_start(
                    out=WB[:, k0:k0 + nk, :],
                    in_=wsrc[:, cc * 128:(cc + 1) * 128].rearrange("(kc p) n -> p kc n", p=128)),
                    w=["WB%d" % i], dma=True)

            def make_merge(cc):
                def f(slot):
                    if cc == 0:
                        for i in range(3):
                            load_wb(0, i)
                    gts = [GN[:, 0:512], GN[:, 512:1024]]
                    gtcs = [gnc(0, 2048), gnc(2048, 2048)]
                    accs = [GN[:, 1024:1536], GN[:, 1536:2048]]
                    acccs = [gnc(4096, 2048), gnc(6144, 2048)]
                    for i, (wsrc, nk, k0, och0, ocells) in enumerate(BRS):
                        for s, (t0, hc) in enumerate(CORE_SLABS):
                            bg = S.banks()
                            mm(psb(bg, 512), [(W[slot][:, kc, i * 128:(i + 1) * 128], hT[:, kc, t0:t0 + 512])
                                              for kc in range(16)], r=WC[slot] + hc, w=["P%d" % bg])
                            S.op("act", lambda hh, s=s, bg=bg: hh.activation(out=gts[s], in_=psb(bg, 512),
                                                                             func=AF.Sigmoid),
                                 r=[], w=gtcs[s] + ["P%d" % bg])
                        for s in range(2):
                            gt, gtc, acc, acc_c = gts[s], gtcs[s], accs[s], acccs[s]
                            bm = S.banks()
                            mm(psb(bm, 512), [(WB[:, k0 + kc, :], OT[:, (och0 + kc) * 1024 + s * 512:
                                                                     (och0 + kc) * 1024 + (s + 1) * 512])
                                              for kc in range(nk)], r=["WB%d" % i] + ocells, w=["P%d" % bm])
                            if i == 0:
                                S.op("dve", lambda hh, gt=gt, bm=bm, acc=acc: hh.tensor_tensor(
                                    out=acc, in0=psb(bm, 512), in1=gt, op=ALU.mult),
                                    r=gtc, w=acc_c + ["P%d" % bm])
                            else:
                                S.op("dve", lambda hh, gt=gt, bm=bm: hh.tensor_tensor(
                                    out=gt, in0=psb(bm, 512), in1=gt, op=ALU.mult),
                                    r=gtc, w=gtc + ["P%d" % bm])
                                if i == 1:
                                    S.op("dve", lambda hh, gt=gt, acc=acc: hh.tensor_tensor(out=acc, in0=acc, in1=gt,
                                                                                            op=ALU.add),
                                         r=gtc + acc_c, w=acc_c)
                                else:
                                    S.op("dve", lambda hh, gt=gt, acc=acc, s=s: hh.tensor_tensor(
                                        out=mT[:, cc, s * 512:(s + 1) * 512], in0=acc, in1=gt, op=ALU.add),
                                        r=gtc + acc_c, w=arc(cc * 2048 + s * 1024, 1024))
                        if cc + 1 < 16:
                            load_wb(cc + 1, i)
                return f
            for cc in range(16):
                rest.append(([(w_in, C_G + cc * 128, 128), (w_in, C_G + 2048 + cc * 128, 128),
                                (w_in, C_G + 4096 + cc * 128, 128)], make_merge(cc)))

            MT_ALL = arc(0, 32768)
            JK_OFF = 32768
            jk = arv(JK_OFF, 2048, BF16)
            jk_c = arc(JK_OFF, 4096)
            c_fst = smalloc(6)
            c_fq = smalloc(48)

            def xo_view(tb):
                return OT[:, tb * 4096:(tb + 1) * 4096].bitcast(F32)

            def make_out(c0, ncol, first, last, bi):
                def f(slot):
                    if first:
                        load_gain(fg_d)
                        for tb in range(8):
                            S.op("sp", lambda hh, tb=tb: hh.dma_start(
                                out=xo_view(tb), in_=xp[r0 + (tb + 1) * 128:r0 + (tb + 2) * 128, :]),
                                w=otc(tb * 8192, 8192), dma=True)
                    for tb in range(8):
                        b = S.banks()
                        mm(psb(b, ncol), [(mT[:, kc, tb * 128:(tb + 1) * 128], W[slot][:, kc, 0:ncol])
                                          for kc in range(16)], r=WC[slot] + MT_ALL, w=["P%d" % b])
                        xs_ = xo_view(tb)[:, c0:c0 + ncol]
                        xc_ = otc(tb * 8192 + c0 * 4, ncol * 4)
                        S.op("dve", lambda hh, xs_=xs_, b=b: hh.tensor_tensor(out=xs_, in0=psb(b, ncol), in1=xs_,
                                                                             op=ALU.add),
                             r=xc_, w=xc_ + ["P%d" % b])
                        fqc = "fq%d_%d" % (tb, bi)
                        S.op("act", lambda hh, xs_=xs_, tb=tb: hh.activation(
                            out=jk[:, 0:ncol], in_=xs_, func=AF.Square,
                            accum_out=SM[:, c_fq + tb * 6 + bi:c_fq + tb * 6 + bi + 1]),
                            r=xc_, w=jk_c + [fqc])
                        if last:
                            i = tb % 2
                            c = c_fst + 3 * i
                            stc = "fst%d" % i
                            xo = xo_view(tb)
                            xoc = otc(tb * 8192, 8192)
                            S.op("dve", lambda hh, c=c, tb=tb: hh.tensor_reduce(
                                out=SM[:, c:c + 1], in_=SM[:, c_fq + tb * 6:c_fq + tb * 6 + 6], axis=AX.X,
                                op=ALU.add), r=["fq%d_%d" % (tb, q) for q in range(6)], w=[stc])
                            S.op("act", lambda hh, c=c: hh.activation(out=SM[:, c + 1:c + 2], in_=SM[:, c:c + 1],
                                                                      func=AF.Sqrt, scale=1.0 / D,
                                                                      bias=SM[:, c_eps:c_eps + 1]),
                                 r=[stc, "eps"], w=[stc])
                            S.op("dve", lambda hh, c=c: hh.reciprocal(out=SM[:, c + 2:c + 3], in_=SM[:, c + 1:c + 2]),
                                 r=[stc], w=[stc])
                            S.op("dve", lambda hh, xo=xo, c=c: hh.scalar_tensor_tensor(
                                out=xo, in0=xo, scalar=SM[:, c + 2:c + 3], in1=GN[:], op0=ALU.mult, op1=ALU.mult),
                                r=xoc + [stc] + GN_ALL, w=xoc)
                            out_toks.append(S.op("act", lambda hh, xo=xo, tb=tb: hh.dma_start(
                                out=y[r0 + tb * 128:r0 + (tb + 1) * 128, :], in_=xo), r=xoc, dma=True))
                return f
            blocks = [(0, 384), (1920, 128), (384, 384), (768, 384), (1152, 384), (1536, 384)]
            for bi, (c0, ncol) in enumerate(blocks):
                rest.append(([(w_out, c0, ncol)], make_out(c0, ncol, bi == 0, bi == len(blocks) - 1, bi)))
            return p0l, zal, zcl, zbl, rest

        out_toks = []
        for tt in range(2):
            p0l, zal, zcl, zbl, rest = tile_stages(tt)
            stages.extend(p0l)
            if tt == 0:
                stages.append(zal[0])
                stages.append((None, mem_prep))
                stages.extend(zal[1:] + zbl + zcl)
                for i, stg in enumerate(rest):
                    stages.append(stg)
                    k = i - 4
                    if 0 <= k < len(kvst):
                        stages.append(kvst[k])
            else:
                stages.extend(zal + zbl + zcl)
                stages.extend(rest)

        widx = [i for i, (segs, _) in enumerate(stages) if segs is not None]
        wpos = {si: k for k, si in enumerate(widx)}
        if widx:
            load_w(0, stages[widx[0]][0])
        for si, (segs, fn) in enumerate(stages):
            if segs is None:
                fn(None)
                continue
            k = wpos[si]
            if k + 1 < len(widx):
                load_w((k + 1) % 2, stages[widx[k + 1]][0])
            fn(k % 2)
        S.wait_all("sp", out_toks)
        S.finish(st)
    return nc


_CACHE = {}


def _consts():
    nd = np.zeros((128, 384), np.float32)
    b = np.arange(128)[:, None].astype(np.float32)
    a = np.arange(128)[None, :].astype(np.float32)
    d_m1 = 128 + b - a
    nd[:, 0:128] = np.where(d_m1 <= 128, -d_m1, BIGNEG)
    nd[:, 128:256] = -np.abs(a - b)
    d_p1 = 128 + a - b
    nd[:, 256:384] = np.where(d_p1 <= 128, -d_p1, BIGNEG)
    ident = np.eye(128, dtype=np.float32)
    return nd.astype(ml_dtypes.bfloat16), ident.astype(ml_dtypes.bfloat16)


def kernel(x, mem, norm_gain, mem_norm_gain, w_in, sink, ln_v_gain, ln_v_bias, w_spatial, b_spatial,
           w_kv_mem, w_br_a, w_br_b, w_br_c, w_out, final_gain):
    f32 = np.float32
    x = np.asarray(x, f32)
    mem = np.asarray(mem, f32)
    B, SEQ, _ = x.shape
    if "nc" not in _CACHE:
        _CACHE["nc"] = build_program()
    nc = _CACHE["nc"]
    nd, ident = _consts()
    w_in2 = np.ascontiguousarray(np.asarray(w_in, f32)[0])
    w_kv2 = np.ascontiguousarray(np.asarray(w_kv_mem, f32)[0])
    w_bra2 = np.ascontiguousarray(np.asarray(w_br_a, f32)[0])
    w_brb2 = np.ascontiguousarray(np.asarray(w_br_b, f32)[0])
    w_brc2 = np.ascontiguousarray(np.asarray(w_br_c, f32)[0])
    w_out2 = np.ascontiguousarray(np.asarray(w_out, f32)[0])
    ng = np.asarray(norm_gain, f32).reshape(1, D)
    mg = np.asarray(mem_norm_gain, f32).reshape(1, D)
    fg = np.asarray(final_gain, f32).reshape(1, D)
    sk = np.asarray(sink, f32).reshape(1, 12)
    lng = np.ascontiguousarray(np.asarray(ln_v_gain, f32).reshape(12, 128).T)
    lnb = np.ascontiguousarray(np.asarray(ln_v_bias, f32).reshape(12, 128).T)
    wsT = np.ascontiguousarray(np.asarray(w_spatial, f32)[0].transpose(2, 0, 1).reshape(128, 1536))
    bs = np.ascontiguousarray(np.asarray(b_spatial, f32)[0].reshape(1, 1536))
    zpad = np.zeros((128, D), f32)
    in_maps = []
    for c in range(NCORES):
        b, half = c // 2, c % 2
        xb = x[b]
        lo = half * 2048 - 128
        hi = half * 2048 + 2048 + 128
        parts = []
        parts.append(zpad if lo < 0 else xb[lo:lo + 128])
        parts.append(xb[half * 2048: half * 2048 + 2048])
        parts.append(zpad if hi > SEQ else xb[hi - 128:hi])
        xpc = np.ascontiguousarray(np.concatenate(parts, axis=0))
        edge = np.zeros((128, 4), f32)
        if half == 0:
            edge[:, 0] = EDGE_NEG
        if half == 1:
            edge[:, 3] = EDGE_NEG
        in_maps.append(dict(xp=xpc, mem=np.ascontiguousarray(mem[b]), w_in=w_in2, w_kv=w_kv2, w_bra=w_bra2,
                            w_brb=w_brb2, w_brc=w_brc2, w_out=w_out2, ng=ng, mg=mg, fg=fg, sink=sk, lng=lng,
                            lnb=lnb, wsT=wsT, bs=bs, edge=edge, nd=nd, ident=ident))
    res = run_bass_kernel_spmd(nc, in_maps, core_ids=list(range(NCORES)))
    out = np.empty((B, SEQ, D), f32)
    for c in range(NCORES):
        b, half = c // 2, c % 2
        out[b, half * 2048:(half + 1) * 2048] = np.asarray(res.results[c]["y"], f32)
    return out
```
